# Optimizing a Trainium2 kernel written in Bass

```python
import math
import jax
import jax.numpy as jnp
from jax import lax
import numpy as np

D_MODEL = 1024
BATCH = 16
SEQ = 256
DEPTH = 4
DEC_BATCH = 8
DEC_SEQ = 1024
PAST_LEN = 256

GRID_W = 64
N_MIXERS = 4
N_DA = (DEPTH + 3) // 4
N_NA = (DEPTH + 2) // 4
N_GQ = (DEPTH + 1) // 4
N_HY = DEPTH // 4
Q_BLOCK = 128
D_FF = 4 * D_MODEL
DA_DH = 64
DA_HEADS = D_MODEL // (2 * DA_DH)
NA_DH = 64
NA_HEADS = D_MODEL // NA_DH
NA_WIN_ROWS = 8
NA_WIN_COLS = 16
GQ_DH = 64
GQ_HEADS = D_MODEL // GQ_DH
GQ_KV_HEADS = 4
GQ_GROUP = GQ_HEADS // GQ_KV_HEADS
HY_ORDER = 2
HY_SHORT = 3
HY_BANDS = 16
HY_EMB = 1 + 2 * HY_BANDS
HY_FFN = 64
HY_DECAY_MIN = 3.07
HY_DECAY_MAX = 15.35
ROPE_BASE = 10000.0
LN_EPS = 1e-5
RMS_EPS = 1e-6
DN_ALPHA = (2 * DEPTH) ** 0.25
DN_BETA = (8 * DEPTH) ** -0.25
NEG_INF = -1e30

kernel_name = "hybrid_diffusion_trunk_step"


def layer_norm(x, g, b):
    xf = x.astype(jnp.float32)
    mu = jnp.mean(xf, axis=-1, keepdims=True)
    var = jnp.mean(jnp.square(xf - mu), axis=-1, keepdims=True)
    return ((xf - mu) * lax.rsqrt(var + LN_EPS) * g + b).astype(x.dtype)


def rms_norm(x, g):
    xf = x.astype(jnp.float32)
    return (xf * lax.rsqrt(jnp.mean(xf * xf, axis=-1, keepdims=True) + RMS_EPS) * g).astype(x.dtype)


def softmax_f32(s):
    return jax.nn.softmax(s.astype(jnp.float32), axis=-1)


def modulate(x, shift, scale):
    return x * (1.0 + scale) + shift


def axial_rope(x):
    L, dh = x.shape[1], x.shape[-1]
    n = dh // 4
    pos = jnp.arange(L)
    inv = ROPE_BASE ** (-jnp.arange(n, dtype=jnp.float32) / n)
    shp = (L,) + (1,) * (x.ndim - 3) + (n,)
    ang_r = ((pos // GRID_W).astype(jnp.float32)[:, None] * inv).reshape(shp)
    ang_c = ((pos % GRID_W).astype(jnp.float32)[:, None] * inv).reshape(shp)
    cr, sr, cc, sc = jnp.cos(ang_r), jnp.sin(ang_r), jnp.cos(ang_c), jnp.sin(ang_c)
    xr1, xr2, xc1, xc2 = jnp.split(x, 4, axis=-1)
    out = jnp.concatenate([xr1 * cr - xr2 * sr, xr1 * sr + xr2 * cr,
                           xc1 * cc - xc2 * sc, xc1 * sc + xc2 * cc], axis=-1)
    return out.astype(x.dtype)


def sweep_query_blocks(fn, q):
    B, L = q.shape[:2]
    nb = L // Q_BLOCK
    qb = jnp.moveaxis(q.reshape((B, nb, Q_BLOCK) + q.shape[2:]), 1, 0)
    ob = lax.map(fn, qb)
    return jnp.moveaxis(ob, 0, 1).reshape((B, L) + ob.shape[3:])


def mha_block(qb, k, v, scale):
    p = softmax_f32(jnp.einsum("bqhd,bkhd->bhqk", qb, k) * scale).astype(v.dtype)
    return jnp.einsum("bhqk,bkhd->bqhd", p, v)


def diff_attention(h_ctx, h_lat, cache_k, cache_v, w_qkv, w_o, lam_p, subln_g, layer_idx):
    lam_init = 0.8 - 0.6 * math.exp(-0.3 * layer_idx)
    lam = (jnp.exp(jnp.sum(lam_p[0] * lam_p[1]).astype(jnp.float32))
           - jnp.exp(jnp.sum(lam_p[2] * lam_p[3]).astype(jnp.float32)) + lam_init)
    scale = DA_DH ** -0.5

    def project(h):
        B, L, _ = h.shape
        q, k, v = jnp.split(h @ w_qkv, 3, axis=-1)
        shp = (B, L, DA_HEADS, 2 * DA_DH)
        return q.reshape(shp), k.reshape(shp), v.reshape(shp)

    def attend(k, v):
        k1, k2 = jnp.split(k, 2, axis=-1)

        def block(qb):
            q1, q2 = jnp.split(qb, 2, axis=-1)
            p1 = softmax_f32(jnp.einsum("bqhd,bkhd->bhqk", q1, k1) * scale)
            p2 = softmax_f32(jnp.einsum("bqhd,bkhd->bhqk", q2, k2) * scale)
            return jnp.einsum("bhqk,bkhd->bqhd", (p1 - lam * p2).astype(v.dtype), v)
        return block

    def finish(o):
        B, L = o.shape[:2]
        o = rms_norm(o, subln_g) * (1.0 - lam_init)
        return o.reshape(B, L, DA_HEADS * 2 * DA_DH) @ w_o

    def rope_pair(x):
        B, L = x.shape[:2]
        return axial_rope(x.reshape(B, L, DA_HEADS, 2, DA_DH)).reshape(x.shape)

    qc, kc, vc = project(h_ctx)
    out_ctx = finish(sweep_query_blocks(attend(kc, vc), qc))
    ql, kl, vl = project(h_lat)
    ql, kl = rope_pair(ql), rope_pair(kl)
    k_all = jnp.concatenate([kl, cache_k], axis=1)
    v_all = jnp.concatenate([vl, cache_v], axis=1)
    out_lat = finish(sweep_query_blocks(attend(k_all, v_all), ql))
    return out_ctx, out_lat, kc, vc


def neighbourhood_attention(h_ctx, h_lat, cache_k, cache_v, w_qkv, w_o, rel_bias):
    scale = NA_DH ** -0.5

    def project(h):
        B, L, _ = h.shape
        q, k, v = jnp.split(h @ w_qkv, 3, axis=-1)
        shp = (B, L, NA_HEADS, NA_DH)
        return q.reshape(shp), k.reshape(shp), v.reshape(shp)

    qc, kc, vc = project(h_ctx)
    Bc, Lc = h_ctx.shape[:2]
    oc = sweep_query_blocks(lambda qb: mha_block(qb, kc, vc, scale), qc)
    out_ctx = oc.reshape(Bc, Lc, NA_HEADS * NA_DH) @ w_o

    ql, kl, vl = project(h_lat)
    B, L = h_lat.shape[:2]
    rows = L // GRID_W
    kr = min(NA_WIN_ROWS, rows)
    grid = (B, rows, GRID_W, NA_HEADS, NA_DH)
    qg, kg, vg = ql.reshape(grid), kl.reshape(grid), vl.reshape(grid)
    row_start = jnp.clip(jnp.arange(rows) - kr // 2, 0, rows - kr)
    cols = jnp.arange(GRID_W)
    col_start = jnp.clip(cols - NA_WIN_COLS // 2, 0, GRID_W - NA_WIN_COLS)
    col_in = ((cols[None, :] >= col_start[:, None])
              & (cols[None, :] < col_start[:, None] + NA_WIN_COLS))
    rel_c = jnp.clip(cols[None, :] - cols[:, None], -(NA_WIN_COLS - 1), NA_WIN_COLS - 1) + NA_WIN_COLS - 1

    def row_block(r):
        rs = row_start[r]
        q_r = lax.dynamic_index_in_dim(qg, r, axis=1, keepdims=False)
        k_r = lax.dynamic_slice_in_dim(kg, rs, kr, axis=1)
        v_r = lax.dynamic_slice_in_dim(vg, rs, kr, axis=1)
        rel_r = rs + jnp.arange(kr) - r + NA_WIN_ROWS - 1
        bias = rel_bias[:, rel_r[None, :, None], rel_c[:, None, :]]
        s_loc = jnp.einsum("bqhd,bikhd->bhqik", q_r, k_r).astype(jnp.float32) * scale + bias
        s_loc = jnp.where(col_in[:, None, :], s_loc, NEG_INF).reshape(B, NA_HEADS, GRID_W, kr * GRID_W)
        s_ctx = jnp.einsum("bqhd,bshd->bhqs", q_r, cache_k).astype(jnp.float32) * scale
        p = jax.nn.softmax(jnp.concatenate([s_loc, s_ctx], axis=-1), axis=-1).astype(v_r.dtype)
        p_loc = p[..., :kr * GRID_W].reshape(B, NA_HEADS, GRID_W, kr, GRID_W)
        p_ctx = p[..., kr * GRID_W:]
        return (jnp.einsum("bhqik,bikhd->bqhd", p_loc, v_r)
                + jnp.einsum("bhqs,bshd->bqhd", p_ctx, cache_v))

    ol = lax.map(row_block, jnp.arange(rows))
    out_lat = jnp.moveaxis(ol, 0, 1).reshape(B, L, NA_HEADS * NA_DH) @ w_o
    return out_ctx, out_lat, kc, vc


def gq_attention(h_ctx, h_lat, cache_k, cache_v, w_qkv, w_o, q_norm, k_norm):
    scale = GQ_DH ** -0.5
    nq, nk = GQ_HEADS * GQ_DH, GQ_KV_HEADS * GQ_DH

    def project(h):
        B, L, _ = h.shape
        u = h @ w_qkv
        q = rms_norm(u[..., :nq].reshape(B, L, GQ_HEADS, GQ_DH), q_norm)
        k = rms_norm(u[..., nq:nq + nk].reshape(B, L, GQ_KV_HEADS, GQ_DH), k_norm)
        v = u[..., nq + nk:].reshape(B, L, GQ_KV_HEADS, GQ_DH)
        return q, k, v

    def attend(k, v):
        def block(qb):
            B, Q = qb.shape[:2]
            qg = qb.reshape(B, Q, GQ_KV_HEADS, GQ_GROUP, GQ_DH)
            p = softmax_f32(jnp.einsum("bqkgd,bskd->bkgqs", qg, k) * scale).astype(v.dtype)
            return jnp.einsum("bkgqs,bskd->bqkgd", p, v).reshape(B, Q, nq)
        return block

    qc, kc, vc = project(h_ctx)
    out_ctx = sweep_query_blocks(attend(kc, vc), qc) @ w_o
    ql, kl, vl = project(h_lat)
    ql, kl = axial_rope(ql), axial_rope(kl)
    k_all = jnp.concatenate([kl, cache_k], axis=1)
    v_all = jnp.concatenate([vl, cache_v], axis=1)
    out_lat = sweep_query_blocks(attend(k_all, v_all), ql) @ w_o
    return out_ctx, out_lat, kc, vc


def hyena_filters_fft(L, w1, b1, w2, b2, freq, w3, log_decay):
    t = jnp.arange(L, dtype=jnp.float32) / L
    ang = 2.0 * math.pi * t[:, None] * jnp.arange(1, HY_BANDS + 1, dtype=jnp.float32)
    emb = jnp.concatenate([t[:, None], jnp.cos(ang), jnp.sin(ang)], axis=-1)
    hid = jnp.sin(freq * (emb @ w1 + b1))
    hid = jnp.sin(freq * (hid @ w2 + b2))
    window = jnp.exp(-jnp.exp(log_decay.astype(jnp.float32)) * t[:, None])
    filt = ((hid @ w3).astype(jnp.float32) * window).reshape(L, HY_ORDER, 2, D_MODEL)
    fwd, bwd = filt[:, :, 0], filt[:, :, 1]
    two_sided = jnp.concatenate(
        [fwd, jnp.zeros((1, HY_ORDER, D_MODEL), jnp.float32), bwd[:0:-1]], axis=0)
    return jnp.fft.rfft(two_sided, axis=0)


def long_conv(z, freq_resp):
    L = z.shape[1]
    zf = jnp.fft.rfft(z.astype(jnp.float32), n=2 * L, axis=1)
    return jnp.fft.irfft(zf * freq_resp[None], n=2 * L, axis=1)[:, :L].astype(z.dtype)


def centred_short_conv(u, w, b):
    L = u.shape[1]
    up = jnp.pad(u, ((0, 0), (HY_SHORT // 2, HY_SHORT // 2), (0, 0)))
    return up[:, :L] * w[0] + up[:, 1:L + 1] * w[1] + up[:, 2:L + 2] * w[2] + b


def hyena(h, w_in, short_w, short_b, w1, b1, w2, b2, freq, w3, log_decay, filter_bias, w_o):
    L = h.shape[1]
    freq_resp = hyena_filters_fft(L, w1, b1, w2, b2, freq, w3, log_decay)
    parts = jnp.split(centred_short_conv(h @ w_in, short_w, short_b), HY_ORDER + 1, axis=-1)
    z = parts[0]
    for o in range(HY_ORDER):
        z = parts[o + 1] * (long_conv(z, freq_resp[:, o]) + z * filter_bias[o])
    return z @ w_o


def squared_relu_mlp(h, w1, w2):
    return jnp.square(jax.nn.relu(h @ w1)) @ w2


def setup_inputs(seed: int = 0) -> dict:
    key = jax.random.key(seed)
    ks = iter(jax.random.split(key, 64))
    d = D_MODEL

    def nrm(shape, scale):
        return jax.random.normal(next(ks), shape, jnp.float32) * scale

    da_w = DA_HEADS * 2 * DA_DH
    na_w = NA_HEADS * NA_DH
    gq_qkv = (GQ_HEADS + 2 * GQ_KV_HEADS) * GQ_DH
    n_filt = 2 * HY_ORDER * d
    return {
        "x_prompt": nrm((BATCH, SEQ, d), 1.0),
        "x_sample": nrm((DEC_BATCH, DEC_SEQ, d), 1.0),
        "c": nrm((DEC_BATCH, d), 1.0),
        "cache_da_k": nrm((DEC_BATCH, N_DA, PAST_LEN, DA_HEADS, 2 * DA_DH), 1.0),
        "cache_da_v": nrm((DEC_BATCH, N_DA, PAST_LEN, DA_HEADS, 2 * DA_DH), 1.0),
        "cache_na_k": nrm((DEC_BATCH, N_NA, PAST_LEN, NA_HEADS, NA_DH), 1.0),
        "cache_na_v": nrm((DEC_BATCH, N_NA, PAST_LEN, NA_HEADS, NA_DH), 1.0),
        "cache_gq_k": nrm((DEC_BATCH, N_GQ, PAST_LEN, GQ_KV_HEADS, GQ_DH), 1.0),
        "cache_gq_v": nrm((DEC_BATCH, N_GQ, PAST_LEN, GQ_KV_HEADS, GQ_DH), 1.0),
        "c_ctx": nrm((d,), 1.0),
        "ada_w": nrm((DEPTH, d, 6 * d), 0.02),
        "ada_b": nrm((DEPTH, 6 * d), 0.02),
        "ln_g": 1.0 + nrm((DEPTH, 2, d), 0.02),
        "ln_b": nrm((DEPTH, 2, d), 0.02),
        "mlp_w1": nrm((DEPTH, d, D_FF), d ** -0.5),
        "mlp_w2": nrm((DEPTH, D_FF, d), DN_BETA * D_FF ** -0.5),
        "da_w_qkv": nrm((N_DA, d, 3 * da_w), d ** -0.5),
        "da_w_o": nrm((N_DA, da_w, d), DN_BETA * da_w ** -0.5),
        "da_lambda": nrm((N_DA, 4, DA_DH), 0.1),
        "da_subln_g": 1.0 + nrm((N_DA, 2 * DA_DH), 0.02),
        "na_w_qkv": nrm((N_NA, d, 3 * na_w), d ** -0.5),
        "na_w_o": nrm((N_NA, na_w, d), DN_BETA * na_w ** -0.5),
        "na_rel_bias": nrm((N_NA, NA_HEADS, 2 * NA_WIN_ROWS - 1, 2 * NA_WIN_COLS - 1), 0.1),
        "gq_w_qkv": nrm((N_GQ, d, gq_qkv), d ** -0.5),
        "gq_w_o": nrm((N_GQ, GQ_HEADS * GQ_DH, d), DN_BETA * (GQ_HEADS * GQ_DH) ** -0.5),
        "gq_q_norm": 1.0 + nrm((N_GQ, GQ_DH), 0.02),
        "gq_k_norm": 1.0 + nrm((N_GQ, GQ_DH), 0.02),
        "hy_w_in": nrm((N_HY, d, (HY_ORDER + 1) * d), d ** -0.5),
        "hy_short_w": nrm((N_HY, HY_SHORT, (HY_ORDER + 1) * d), 0.5),
        "hy_short_b": nrm((N_HY, (HY_ORDER + 1) * d), 0.02),
        "hy_ffn_w1": nrm((N_HY, HY_EMB, HY_FFN), HY_EMB ** -0.5),
        "hy_ffn_b1": nrm((N_HY, HY_FFN), 0.02),
        "hy_ffn_w2": nrm((N_HY, HY_FFN, HY_FFN), HY_FFN ** -0.5),
        "hy_ffn_b2": nrm((N_HY, HY_FFN), 0.02),
        "hy_ffn_freq": 1.0 + nrm((N_HY, HY_FFN), 0.02),
        "hy_ffn_w3": nrm((N_HY, HY_FFN, n_filt), 0.1 * HY_FFN ** -0.5),
        "hy_log_decay": jnp.log(jnp.linspace(HY_DECAY_MIN, HY_DECAY_MAX, n_filt))[None, :]
                        + nrm((N_HY, n_filt), 0.01),
        "hy_filter_bias": nrm((N_HY, HY_ORDER, d), 0.1),
        "hy_w_o": nrm((N_HY, d, d), DN_BETA * d ** -0.5),
    }


def reference(x_prompt, x_sample, c, cache_da_k, cache_da_v, cache_na_k, cache_na_v,
              cache_gq_k, cache_gq_v, c_ctx, ada_w, ada_b, ln_g, ln_b, mlp_w1, mlp_w2,
              da_w_qkv, da_w_o, da_lambda, da_subln_g, na_w_qkv, na_w_o, na_rel_bias,
              gq_w_qkv, gq_w_o, gq_q_norm, gq_k_norm, hy_w_in, hy_short_w, hy_short_b,
              hy_ffn_w1, hy_ffn_b1, hy_ffn_w2, hy_ffn_b2, hy_ffn_freq, hy_ffn_w3,
              hy_log_decay, hy_filter_bias, hy_w_o):
    xp, xs = x_prompt, x_sample
    silu_ctx = jax.nn.silu(c_ctx)[None, :]
    silu_c = jax.nn.silu(c)
    da_k, da_v, na_k, na_v, gq_k, gq_v = [], [], [], [], [], []
    for i in range(DEPTH):
        m, j = i % N_MIXERS, i // N_MIXERS
        mod_p = jnp.split((silu_ctx @ ada_w[i] + ada_b[i])[:, None, :], 6, axis=-1)
        mod_s = jnp.split((silu_c @ ada_w[i] + ada_b[i])[:, None, :], 6, axis=-1)
        hp = modulate(xp, mod_p[0], mod_p[1])
        hs = modulate(xs, mod_s[0], mod_s[1])
        if m == 0:
            op, os_, kc, vc = diff_attention(hp, hs, cache_da_k[:, j], cache_da_v[:, j],
                                             da_w_qkv[j], da_w_o[j], da_lambda[j], da_subln_g[j], i)
            da_k.append(kc)
            da_v.append(vc)
        elif m == 1:
            op, os_, kc, vc = neighbourhood_attention(hp, hs, cache_na_k[:, j], cache_na_v[:, j],
                                                      na_w_qkv[j], na_w_o[j], na_rel_bias[j])
            na_k.append(kc)
            na_v.append(vc)
        elif m == 2:
            op, os_, kc, vc = gq_attention(hp, hs, cache_gq_k[:, j], cache_gq_v[:, j],
                                           gq_w_qkv[j], gq_w_o[j], gq_q_norm[j], gq_k_norm[j])
            gq_k.append(kc)
            gq_v.append(vc)
        else:
            hy_args = (hy_w_in[j], hy_short_w[j], hy_short_b[j], hy_ffn_w1[j], hy_ffn_b1[j],
                       hy_ffn_w2[j], hy_ffn_b2[j], hy_ffn_freq[j], hy_ffn_w3[j],
                       hy_log_decay[j], hy_filter_bias[j], hy_w_o[j])
            op = hyena(hp, *hy_args)
            os_ = hyena(hs, *hy_args)
        xp = layer_norm(DN_ALPHA * xp + mod_p[2] * op, ln_g[i, 0], ln_b[i, 0])
        xs = layer_norm(DN_ALPHA * xs + mod_s[2] * os_, ln_g[i, 0], ln_b[i, 0])
        fp = squared_relu_mlp(modulate(xp, mod_p[3], mod_p[4]), mlp_w1[i], mlp_w2[i])
        fs = squared_relu_mlp(modulate(xs, mod_s[3], mod_s[4]), mlp_w1[i], mlp_w2[i])
        xp = layer_norm(DN_ALPHA * xp + mod_p[5] * fp, ln_g[i, 1], ln_b[i, 1])
        xs = layer_norm(DN_ALPHA * xs + mod_s[5] * fs, ln_g[i, 1], ln_b[i, 1])
    state_da_k = jnp.stack(da_k, axis=1)
    state_da_v = jnp.stack(da_v, axis=1)
    state_na_k = jnp.stack(na_k, axis=1)
    state_na_v = jnp.stack(na_v, axis=1)
    state_gq_k = jnp.stack(gq_k, axis=1)
    state_gq_v = jnp.stack(gq_v, axis=1)
    return (xp, xs, state_da_k, state_da_v, state_na_k, state_na_v, state_gq_k, state_gq_v)
```

```python
import numpy as np
import ml_dtypes
from contextlib import ExitStack
import concourse.bass as bass
import concourse.mybir as mybir
from concourse.bass_utils import run_bass_kernel_spmd

F32 = mybir.dt.float32
BF16 = mybir.dt.bfloat16
AF = mybir.ActivationFunctionType
ALU = mybir.AluOpType
AX = mybir.AxisListType

SEM_LIMIT = 30000
NRING = 8


class _Eng:
    def __init__(self, kb, name, handle):
        self.kb, self.name, self.h = kb, name, handle
        self.cnt = 0
        self.seen = {}
        self.cur = None
        self.new_sem()

    def new_sem(self):
        self.cur = self.kb.add_sem(self.name, self.name)
        self.cnt = 0


class Sched:
    def __init__(self, nc, es):
        self.nc, self.es = nc, es
        self.allsems, self.owner = [], []
        self.E = {}
        for name, h in (("pe", nc.tensor), ("act", nc.scalar), ("dve", nc.vector),
                        ("pool", nc.gpsimd), ("sp", nc.sync)):
            self.E[name] = _Eng(self, name, h)
        self.rings = {}
        for q in ("pool", "sp"):
            self.rings[q] = dict(n=0, sems=[self.add_sem("ring_" + q, "dma") for _ in range(NRING)])
        self.lastw, self.readers = {}, {}
        self.fence = {}
        self.ninstr = 0

    def add_sem(self, name, owner):
        s = self.es.enter_context(self.nc.semaphore(f"s_{name}_{len(self.allsems)}"))
        self.allsems.append(s)
        self.owner.append(owner)
        return len(self.allsems) - 1

    def _wait(self, E, tok):
        si, v = tok
        if E.seen.get(si, 0) >= v:
            return
        E.h.wait_ge(self.allsems[si], v)
        E.seen[si] = v

    def _deps(self, rd, wr, dma=False):
        toks = {}

        def add(t):
            if toks.get(t[0], 0) < t[1]:
                toks[t[0]] = t[1]
        for k in rd:
            w = self.lastw.get(k)
            if w is not None:
                add(w)
        for k in wr:
            w = self.lastw.get(k)
            fresh = w is None and k not in self.readers
            if w is not None:
                add(w)
            for si, v in self.readers.get(k, {}).items():
                add((si, v))
            if fresh and dma:
                for si, v in self.fence.items():
                    add((si, v))
        return toks

    def _record(self, tok, rd, wr):
        for k in rd:
            d = self.readers.setdefault(k, {})
            if d.get(tok[0], 0) < tok[1]:
                d[tok[0]] = tok[1]
        for k in wr:
            self.lastw[k] = tok
            self.readers[k] = {}

    def op(self, eng, fn, rd=(), wr=(), inc=True):
        E = self.E[eng]
        for si, v in self._deps(rd, wr).items():
            if eng == "pe" and self.owner[si] == "pe":
                continue
            self._wait(E, (si, v))
        ins = fn(E.h)
        self.ninstr += 1
        if inc:
            E.cnt += 1
            ins.then_inc(self.allsems[E.cur], 1)
            tok = (E.cur, E.cnt)
            if E.cnt >= SEM_LIMIT:
                E.new_sem()
        else:
            tok = (E.cur, E.cnt + 1)
        self._record(tok, rd, wr)
        return ins

    def dma(self, q, out, in_, rd=(), wr=()):
        E = self.E[q]
        ring = self.rings[q]
        n = ring["n"]
        slot = n % NRING
        for si, v in self._deps(rd, wr, dma=True).items():
            self._wait(E, (si, v))
        if n >= NRING:
            self._wait(E, (ring["sems"][slot], 16 * (n // NRING)))
        E.h.dma_start(out=out, in_=in_).then_inc(self.allsems[ring["sems"][slot]], 16)
        self.ninstr += 1
        tok = (ring["sems"][slot], 16 * (n // NRING + 1))
        ring["n"] = n + 1
        self._record(tok, rd, wr)

    def retire(self, pred):
        toks = {}
        keys = [k for k in set(self.lastw) | set(self.readers) if pred(k)]
        for k in keys:
            w = self.lastw.get(k)
            if w is not None and toks.get(w[0], 0) < w[1]:
                toks[w[0]] = w[1]
            for si, v in self.readers.get(k, {}).items():
                if toks.get(si, 0) < v:
                    toks[si] = v
            self.lastw.pop(k, None)
            self.readers.pop(k, None)
        for name in ("pe", "act", "dve"):
            E = self.E[name]
            for si, v in toks.items():
                if name == "pe" and self.owner[si] == "pe":
                    continue
                self._wait(E, (si, v))
        for si, v in toks.items():
            if self.fence.get(si, 0) < v:
                self.fence[si] = v

    def finish(self):
        E = self.E["sp"]
        for q in ("pool", "sp"):
            ring = self.rings[q]
            n = ring["n"]
            for slot in range(NRING):
                cnt = (n - slot + NRING - 1) // NRING if n > slot else 0
                if cnt > 0:
                    self._wait(E, (ring["sems"][slot], 16 * cnt))
        for name in ("pe", "act", "dve", "pool"):
            e = self.E[name]
            if e.cnt > 0:
                self._wait(E, (e.cur, e.cnt))


D = 1024
NT = 12
T = NT * 128
DFF = 4096
DEPTH = 4
ALPHA = (2 * DEPTH) ** 0.25
LN_EPS = 1e-5 / (ALPHA * ALPHA)
RMS_EPS = 1e-6
NSLOT = 3
SEQS = [(0, 2, "p"), (2, 2, "p"), (4, 8, "s")]


class KB:
    def __init__(self, n_layers=4):
        self.n_layers = n_layers
        self.nc = nc = bass.Bass("TRN2", target_bir_lowering=False)
        self.es = ExitStack()
        self.S = Sched(nc, self.es)
        self.bank_i = 0
        self.ring_i = 0
        self.dram = {}

    def din(self, name, shape):
        self.dram[name] = self.nc.dram_tensor(name, list(shape), F32, kind="ExternalInput").ap()
        return self.dram[name]

    def dout(self, name, shape):
        self.dram[name] = self.nc.dram_tensor(name, list(shape), F32, kind="ExternalOutput").ap()
        return self.dram[name]

    def sb(self, name, shape, dt, es=None):
        self.sb_i = getattr(self, "sb_i", 0) + 1
        return (es or self.es).enter_context(self.nc.sbuf_tensor(f"{name}_{self.sb_i}", list(shape), dt))

    def newbank(self):
        b = self.bank_i % 7
        self.bank_i += 1
        return b

    def op(self, eng, fn, rd=(), wr=(), inc=True):
        if eng == "pe":
            self.npe = getattr(self, "npe", 0) + 1
        return self.S.op(eng, fn, rd, wr, inc)

    def mark(self, name):
        if not hasattr(self, "marks"):
            self.marks = []
        self.marks.append((name, getattr(self, "npe", 0)))

    def mm(self, out, pairs, rd, wr):
        n = len(pairs)
        for j, (l, r) in enumerate(pairs):
            self.op("pe", lambda e, l=l, r=r, j=j: e.matmul(out, lhsT=l, rhs=r, start=(j == 0), stop=(j == n - 1)),
                    rd=rd, wr=wr, inc=(j == n - 1))

    def wload(self, src, ncols, nk=8, pin=False):
        while True:
            self.ring_i = (self.ring_i + 1) % len(self.slots)
            slot, key = self.slots[self.ring_i]
            if key not in self.pinned:
                break
        if pin:
            self.pinned.add(key)
        self.S.dma("pool", slot[:, 0:nk, :ncols], src.rearrange("(kc p) n -> p kc n", p=128), wr=[key])
        return slot, key

    def transposes(self, src_bf, nblk, rd, dst_fn, dst_keys, eng="act"):
        b = self.newbank()
        pb = self.PSB[b]
        for j in range(nblk):
            self.op("pe", lambda e, j=j: e.transpose(pb[:, j * 128:(j + 1) * 128], src_bf[:, j * 128:(j + 1) * 128], self.ident[:]),
                    rd=list(rd) + ["ident"], wr=[("ps", b)], inc=(j == nblk - 1))
        dst_fn(pb[:, 0:nblk * 128].rearrange("p (c t) -> p c t", c=nblk), ("ps", b))

    def mmg(self, out, pairs, rd, wr, start=True, stop=True):
        n = len(pairs)
        for j, (l, r) in enumerate(pairs):
            self.op("pe", lambda e, l=l, r=r, j=j: e.matmul(out, lhsT=l, rhs=r, start=(start and j == 0),
                                                         stop=(stop and j == n - 1)),
                    rd=rd, wr=wr, inc=(j == n - 1))

    def setup(self):
        nc, S = self.nc, self.S
        din, dout, sb = self.din, self.dout, self.sb
        xin = din("xin", [T, D])
        din("cvec", [128, 16]); din("ident", [128, 128]); din("idlo", [128, 128]); din("idhi", [128, 128])
        din("neg", [128, 64]); din("sel", [2, 256]); din("ropec", [128, 512]); din("ropes", [128, 512])
        din("ada_w", [4, D, 6 * D]); din("adab_col", [4, 128, 96]); din("ada_b", [4, 6 * D])
        din("ln_g", [4, 2, D]); din("ln_b", [4, 2, D]); din("lng_col", [128, 64]); din("lnb_col", [128, 64])
        din("mlp_w1", [4, D, DFF]); din("mlp_w2", [4, DFF, D])
        din("da_w_qkv", [D, 3 * D]); din("da_w_o", [D, D]); din("da_lambda", [1, 256]); din("da_subln_g", [1, 128])
        din("ck_da", [256, D]); din("cv_da", [256, D])
        din("na_w_qkv", [D, 3 * D]); din("na_w_o", [D, D]); din("na_tab", [128, 16 * 2048])
        din("ck_na", [256, D]); din("cv_na", [256, D])
        din("gq_w_qkv", [D, 1536]); din("gq_w_o", [D, D]); din("gq_q_norm", [1, 64]); din("gq_k_norm", [1, 64])
        din("ck_gq", [256, 256]); din("cv_gq", [256, 256])
        din("hy_w_in", [D, 3 * D]); din("hy_w_o", [D, D]); din("hy_sw_col", [128, 72]); din("hy_sb_col", [128, 24])
        din("hy_fb_col", [128, 16]); din("hy_w1", [33, 64]); din("hy_w2", [64, 64]); din("hy_w3", [64, 4096])
        din("hy_cols", [64, 3]); din("hy_log_decay", [1, 4096])
        for Ln in (256, 1024):
            din(f"embT{Ln}", [33, Ln]); din(f"tcol{Ln}", [128, Ln // 128])
            for m in ("Cm", "Sm", "CmT", "SmT"):
                self.dram[f"{m}{Ln}"] = self.nc.dram_tensor(f"{m}{Ln}", [Ln, Ln], BF16, kind="ExternalInput").ap()
            self.dram[f"H{Ln}"] = self.nc.dram_tensor(f"Hscr{Ln}", [2, 2, Ln, D], BF16, kind="Internal").ap()
        dout("y", [T, D])
        for n in ("da", "na"):
            dout(f"st_{n}_k", [512, D]); dout(f"st_{n}_v", [512, D])
        dout("st_gq_k", [512, 256]); dout("st_gq_v", [512, 256])
        d = self.dram
        self.PSall = self.es.enter_context(nc.psum_tensor("psall", [128, 8, 512], F32))
        self.PS = [self.PSall[:, i, :] for i in range(8)]
        self.PSB = [self.PSall[:, i, :].bitcast(BF16) for i in range(8)]
        self.pair_i = 0
        self.x = sb("x", [128, NT, D], F32)
        self.hT = sb("hT", [128, 8, T], BF16)
        self.ring = sb("ring", [128, NSLOT, 8, 512], BF16)
        self.slots = [(self.ring[:, s_], ("ring", s_)) for s_ in range(NSLOT)]
        self.pinned = set()
        self.gate_bc = sb("gate_bc", [128, 2, D], F32)
        self.lnbc = sb("lnbc", [128, 2, D], F32)
        self.ident = sb("ident_sb", [128, 128], BF16)
        self.sel = sb("sel_sb", [2, 256], F32)
        self.cvec = sb("cvec_sb", [128, 16], F32)
        self.siluT = sb("siluT", [128, 16], BF16)
        self.modT = sb("modT", [128, 96], F32)
        self.adabc = sb("adabc", [128, 96], F32)
        self.lngc = sb("lngc", [128, 64], F32)
        self.lnbcol = sb("lnbcol", [128, 64], F32)
        self.colv = sb("colv", [128, 2, 2, 16], F32)
        self.onep = sb("onep", [128, 16], F32)
        self.grow = sb("grow", [2, 2, D], F32)
        self.xnb = sb("xnb", [128, 4, D], BF16)
        self.lnq = []
        self.lazyq = []
        self.lnst = sb("lnst", [128, 2, 2, 6], F32)
        self.lnmv = sb("lnmv", [128, 2, 4], F32)
        self.ln_i = 0
        self.held = set()
        S.dma("pool", self.ident[:], d["ident"], wr=["ident"])
        S.dma("sp", self.sel[:], d["sel"], wr=["sel"])
        S.dma("sp", self.cvec[:], d["cvec"], wr=["cvec"])
        S.dma("sp", self.lngc[:], d["lng_col"], wr=["lngc"])
        S.dma("sp", self.lnbcol[:], d["lnb_col"], wr=["lnbcol"])
        for t in range(NT):
            S.dma("sp", self.x[:, t, :], xin[t * 128:(t + 1) * 128, :], wr=[("x", t)])
        self.op("act", lambda e: e.activation(self.siluT[:], self.cvec[:], AF.Silu), rd=["cvec"], wr=["siluT"])

    def newbank(self):
        while True:
            b = self.bank_i % 7
            self.bank_i += 1
            if b not in self.held:
                return b

    def newpair(self):
        while True:
            b = (self.pair_i % 3) * 2
            self.pair_i += 1
            if b not in self.held and (b + 1) not in self.held:
                return b, self.PSall[:, b:b + 2, :].rearrange("p a n -> p (a n)")

    def ada_gen(self, i):
        d = self.dram
        S = self.S
        siluT = self.siluT[:].rearrange("p (k g) -> p k g", g=2)
        S.dma("sp", self.adabc[:], d["adab_col"][i], wr=["adabc"])
        bT = 7
        for j in range(12):
            slot, rk = self.wload(d["ada_w"][i][:, 512 * j:512 * (j + 1)], 512)
            piece = j // 2
            if piece in (2, 5):
                gi = 0 if piece == 2 else 1
                hf = j % 2
                c0 = 2 * D if gi == 0 else 5 * D
                S.dma("sp", self.grow[:, gi, hf * 512:(hf + 1) * 512],
                      d["ada_b"][i:i + 1, c0 + hf * 512:c0 + (hf + 1) * 512].partition_broadcast(2), wr=[("grow", gi, hf)])
                b = self.newbank()
                self.mmg(self.PS[b][0:2, :], [(siluT[:, kc, :], slot[:, kc, :]) for kc in range(8)],
                         rd=["siluT", rk], wr=[("ps", b)])
                dst = self.grow[:, gi, hf * 512:(hf + 1) * 512]
                self.op("dve", lambda e, b=b, dst=dst, gi=gi, hf=hf: e.tensor_tensor(
                    dst, self.PS[b][0:2, :], dst, ALU.add),
                    rd=[("ps", b), ("grow", gi, hf)], wr=[("grow", gi, hf)])
                self.op("dve", lambda e, dst=dst: e.tensor_scalar_mul(dst, dst, 1.0 / ALPHA),
                        rd=[("grow", gi, hf)], wr=[("grow", gi, hf)])
            else:
                for mb in range(4):
                    fc = 4 * j + mb
                    self.mmg(self.PS[bT][:, 2 * fc:2 * fc + 2],
                             [(slot[:, kc, mb * 128:(mb + 1) * 128], siluT[:, kc, :]) for kc in range(8)],
                             rd=["siluT", rk], wr=[("ps", bT)])
                if j in (3, 9):
                    c0 = 0 if j == 3 else 48
                    self.op("dve", lambda e, c0=c0: e.tensor_tensor(
                        self.modT[:, c0:c0 + 32], self.PS[bT][:, c0:c0 + 32], self.adabc[:, c0:c0 + 32], ALU.add),
                        rd=[("ps", bT), "adabc"], wr=[("modT", c0)])
                    self.colvecs(i, 0 if j == 3 else 1)
            yield j

    def ada0_tick(self):
        if getattr(self, "ada0_it", None) is not None:
            if next(self.ada0_it, None) is None:
                self.ada0_it = None

    def ada_tick(self, limit, n=1):
        for _ in range(n):
            if self.ada_it is not None and self.ada_steps < limit:
                self.ada_steps += 1
                if next(self.ada_it, None) is None:
                    self.ada_it = None

    def colvecs(self, i, slot):
        c0 = 0 if slot == 0 else 48
        mod = self.modT[:, c0:c0 + 32].rearrange("p (a c g) -> p a g c", a=2, g=2)
        onep = self.onep[:].rearrange("p (g c) -> p g c", g=2)
        G = self.colv[:, slot, 0, :].rearrange("p (g c) -> p g c", g=2)
        B = self.colv[:, slot, 1, :].rearrange("p (g c) -> p g c", g=2)
        kk = ("colv", slot)
        self.op("dve", lambda e: e.tensor_scalar_add(onep, mod[:, 1], 1.0), rd=[("modT", c0)], wr=["onep"])
        if slot == 0 and i == 0:
            self.op("dve", lambda e: e.tensor_copy(G, onep), rd=["onep"], wr=[kk])
            self.op("dve", lambda e: e.tensor_copy(B, mod[:, 0]), rd=[("modT", c0), kk], wr=[kk])
            return
        li, lj = (i - 1, 1) if slot == 0 else (i, 0)
        o = (li * 2 + lj) * 8
        gcol = self.lngc[:, o:o + 8].unsqueeze(1).broadcast_to([128, 2, 8])
        bcol = self.lnbcol[:, o:o + 8].unsqueeze(1).broadcast_to([128, 2, 8])
        self.op("dve", lambda e: e.tensor_tensor(G, onep, gcol, ALU.mult), rd=["onep", "lngc"], wr=[kk])
        self.op("dve", lambda e: e.tensor_tensor(B, onep, bcol, ALU.mult), rd=["onep", "lnbcol", kk], wr=[kk])
        self.op("dve", lambda e: e.tensor_tensor(B, B, mod[:, 0], ALU.add), rd=[kk, ("modT", c0)], wr=[kk])

    def gate_bcast(self, gi):
        for g in range(2):
            for hf in range(2):
                b = self.newbank()
                self.mmg(self.PS[b][:], [(self.sel[:, g * 128:(g + 1) * 128], self.grow[:, gi, hf * 512:(hf + 1) * 512])],
                         rd=["sel", ("grow", gi, hf)], wr=[("ps", b)])
                self.op("act", lambda e, b=b, g=g, hf=hf: e.activation(
                    self.gate_bc[:, g, hf * 512:(hf + 1) * 512], self.PS[b][:], AF.Copy),
                    rd=[("ps", b)], wr=[("gate", g)])

    def load_lnbc(self, i, j):
        d = self.dram
        self.S.dma("sp", self.lnbc[:, 0, :], d["ln_g"][i][j:j + 1, :].partition_broadcast(128), wr=["lnbc"])
        self.S.dma("sp", self.lnbc[:, 1, :], d["ln_b"][i][j:j + 1, :].partition_broadcast(128), wr=["lnbc"])

    def to_hT(self, xnb, kx, t, slot):
        g = 0 if t < 4 else 1
        G = self.colv[:, slot, 0, g * 8:(g + 1) * 8]
        B = self.colv[:, slot, 1, g * 8:(g + 1) * 8]

        def dst(pv, pk):
            for c in range(8):
                self.op("act", lambda e, c=c: e.activation(self.hT[:, c, t * 128:(t + 1) * 128], pv[:, c, :], AF.Identity,
                                                           bias=B[:, c:c + 1], scale=G[:, c:c + 1]),
                        rd=[pk, ("colv", slot)], wr=[("hT", t)])
        self.transposes(xnb, 8, [kx], dst, None)

    def first_h(self):
        for t in range(NT):
            p = self.ln_i % 4
            self.ln_i += 1
            xnb, kx = self.xnb[:, p, :], ("xnb", p)
            self.op("act", lambda e, xnb=xnb, t=t: e.activation(xnb, self.x[:, t, :], AF.Copy), rd=[("x", t)], wr=[kx])
            self.ln_push(lambda xnb=xnb, kx=kx, t=t: self.to_hT(xnb, kx, t, 0))
        self.ln_flush()

    def accum_gen(self, t, o_aps, o_keys):
        g = 0 if t < 4 else 1
        kxt = ("x", t)
        for h in range(2):
            self.op("dve", lambda e, h=h: e.tensor_tensor(o_aps[h], o_aps[h], self.gate_bc[:, g, h * 512:(h + 1) * 512], ALU.mult),
                    rd=[o_keys[h], ("gate", g)], wr=[o_keys[h]])
            yield
        for h in range(2):
            xh = self.x[:, t, h * 512:(h + 1) * 512]
            self.op("dve", lambda e, h=h, xh=xh: e.tensor_tensor(xh, o_aps[h], xh, ALU.add), rd=[o_keys[h], kxt], wr=[kxt])
            yield

    def accum_x(self, t, o_aps, o_keys):
        for _ in self.accum_gen(t, o_aps, o_keys):
            pass

    def ln_gen(self, t, o_aps, o_keys, slot, last=False):
        p = self.ln_i % 2
        p3 = self.ln_i % 4
        self.ln_i += 1
        xt, kxt = self.x[:, t, :], ("x", t)
        xnb, kx = self.xnb[:, p3, :], ("xnb", p3)
        st, mv = self.lnst[:, p], self.lnmv[:, p]
        ks = ("lnst", p)
        if o_aps is not None:
            yield from self.accum_gen(t, o_aps, o_keys)
        for h in range(2):
            self.op("dve", lambda e, h=h: e.bn_stats(st[:, h, :], xt[:, h * 512:(h + 1) * 512]), rd=[kxt], wr=[ks])
            yield
        self.op("dve", lambda e: e.bn_aggr(mv[:, 0:2], st), rd=[ks], wr=[ks])
        yield
        self.op("dve", lambda e: e.tensor_scalar_add(mv[:, 2:3], mv[:, 1:2], LN_EPS), rd=[ks], wr=[ks])
        yield
        self.op("act", lambda e: e.sqrt(mv[:, 2:3], mv[:, 2:3]), rd=[ks], wr=[ks])
        yield
        self.op("dve", lambda e: e.reciprocal(mv[:, 2:3], mv[:, 2:3]), rd=[ks], wr=[ks])
        yield
        self.op("dve", lambda e: e.tensor_scalar(mv[:, 3:4], mv[:, 0:1], mv[:, 2:3], -1.0, ALU.mult, ALU.mult),
                rd=[ks], wr=[ks])
        yield
        if not last:
            self.op("act", lambda e: e.activation(xnb, xt, AF.Identity, bias=mv[:, 3:4], scale=mv[:, 2:3]),
                    rd=[kxt, ks], wr=[kx])
            yield
        self.op("act", lambda e: e.activation(xt, xt, AF.Identity, bias=mv[:, 3:4], scale=mv[:, 2:3]),
                rd=[kxt, ks], wr=[kxt])
        yield
        def affine():
            self.op("dve", lambda e: e.tensor_tensor(xt, xt, self.lnbc[:, 0, :], ALU.mult), rd=[kxt, "lnbc"], wr=[kxt])
            self.op("dve", lambda e: e.tensor_tensor(xt, xt, self.lnbc[:, 1, :], ALU.add), rd=[kxt, "lnbc"], wr=[kxt])
        if last:
            affine()
            yield
            self.S.dma("sp", self.dram["y"][t * 128:(t + 1) * 128, :], xt, rd=[kxt])
        else:
            self.lazyq.append(affine)
            self.ln_push(lambda: self.to_hT(xnb, kx, t, slot))

    def ln_tiles(self, items):
        gens = [self.ln_gen(*it) for it in items]
        while gens:
            for g_ in list(gens):
                try:
                    next(g_)
                except StopIteration:
                    gens.remove(g_)

    def ln_tile(self, t, o_aps, o_keys, slot, last=False):
        self.ln_tiles([(t, o_aps, o_keys, slot, last)])

    def lazy_flush(self, n=None):
        while self.lazyq and (n is None or n > 0):
            self.lazyq.pop(0)()
            if n is not None:
                n -= 1

    def ln_push(self, fn):
        self.lnq.append(fn)
        if len(self.lnq) > 2:
            self.lnq.pop(0)()

    def ln_flush(self):
        while self.lnq:
            self.lnq.pop(0)()

    def mlp_phase(self, i, ada_it, last):
        d = self.dram
        ph = ExitStack()
        uT = self.sb("uT", [128, 8, T], BF16, ph)
        r32 = self.sb("r32", [128, 2, 512], F32, ph)
        ring2 = self.sb("ring2", [128, 4, 8, 512], BF16, ph)
        self.slots = self.slots[:NSLOT] + [(ring2[:, s_], ("ring2", s_)) for s_ in range(4)]
        self.ring_i = len(self.slots) - 1
        self.gate_bcast(1)
        w1, w2 = d["mlp_w1"][i], d["mlp_w2"][i]
        ri = 0

        def ada_step():
            self.ada_tick(12)
        for fb in range(4):
            for cc in range(2):
                c = 2 * fb + cc
                slot, rk = self.wload(w1[:, 512 * c:512 * (c + 1)], 512)
                for tb in range(3):
                    if tb == 2:
                        self.ln_flush()
                    hk = [("hT", tb * 4 + q) for q in range(4)]
                    for mb in range(4):
                        self.lazy_flush(1)
                        b = self.newbank()
                        self.mmg(self.PS[b][:],
                                 [(slot[:, kc, mb * 128:(mb + 1) * 128], self.hT[:, kc, tb * 512:(tb + 1) * 512]) for kc in range(8)],
                                 rd=hk + [rk], wr=[("ps", b)])
                        rr, kr = r32[:, ri % 2, :], ("r32", ri % 2)
                        ri += 1
                        self.op("act", lambda e, b=b, rr=rr: e.activation(rr, self.PS[b][:], AF.Relu),
                                rd=[("ps", b)], wr=[kr])
                        ud = uT[:, cc * 4 + mb, tb * 512:(tb + 1) * 512]
                        self.op("dve", lambda e, rr=rr, ud=ud: e.tensor_tensor(ud, rr, rr, ALU.mult),
                                rd=[kr], wr=[("uT", cc * 4 + mb, tb)])
                ada_step()
            if fb == 0:
                self.lazy_flush()
                self.load_lnbc(i, 1)
            sl2 = [self.wload(w2[fb * 1024:(fb + 1) * 1024, hf * 512:(hf + 1) * 512], 512) for hf in range(2)]
            items = []
            for t in range(NT):
                bs = []
                uk = [("uT", kc, t // 4) for kc in range(8)]
                for hf in range(2):
                    slot, rk = sl2[hf]
                    b = self.newbank()
                    self.held.add(b)
                    self.mmg(self.PS[b][:], [(uT[:, kc, t * 128:(t + 1) * 128], slot[:, kc, :]) for kc in range(8)],
                             rd=uk + [rk], wr=[("ps", b)])
                    bs.append(b)
                items.append((t, [self.PS[b][:] for b in bs], [("ps", b) for b in bs], 0, last))
                if len(items) == 2:
                    if fb == 3:
                        self.ln_tiles(items)
                    else:
                        gens = [self.accum_gen(it[0], it[1], it[2]) for it in items]
                        while gens:
                            for g_ in list(gens):
                                try:
                                    next(g_)
                                except StopIteration:
                                    gens.remove(g_)
                    for it in items:
                        for kb_ in it[2]:
                            self.held.discard(kb_[1])
                    items = []
            ada_step()
        self.S.retire(lambda k: isinstance(k, tuple) and k[0] in ("uT", "r32", "ring2"))
        self.slots = self.slots[:NSLOT]
        self.ring_i = self.ring_i % NSLOT
        ph.close()

    ATT = {
        "da": dict(w="da_w_qkv", wo="da_w_o", ck="ck_da", cv="cv_da", nh=8, dv=128, nvh=8,
                   chunks=[("q", 0, 512), ("q", 512, 512), ("k", 1024, 512), ("k", 1536, 512),
                           ("v", 2048, 512), ("v", 2560, 512)]),
        "na": dict(w="na_w_qkv", wo="na_w_o", ck="ck_na", cv="cv_na", nh=16, dv=64, nvh=16,
                   chunks=[("q", 0, 512), ("q", 512, 512), ("k", 1024, 512), ("k", 1536, 512),
                           ("v", 2048, 512), ("v", 2560, 512)]),
        "gq": dict(w="gq_w_qkv", wo="gq_w_o", ck="ck_gq", cv="cv_gq", nh=16, dv=64, nvh=4,
                   chunks=[("q", 0, 512), ("q", 512, 512), ("kv", 1024, 512)]),
    }

    def rope(self, src, skeys, lt, dst, dkey):
        A, B = self.rtA[:], self.rtB[:]
        cosb = self.ropec[:, lt, :].unsqueeze(1).broadcast_to([128, 8, 64])
        v8 = lambda ap: ap.rearrange("p (g f) -> p g f", g=8)
        v5 = lambda ap: ap.rearrange("p (g a h j) -> p g a h j", g=8, a=2, h=2, j=16)
        sinv = self.ropes[:, lt, :].rearrange("p (a h j) -> p a h j", a=2, h=2, j=16)
        hA = [("rtAh", 0), ("rtAh", 1)]
        hB = [("rtBh", 0), ("rtBh", 1)]
        self.op("dve", lambda e: e.tensor_tensor(v8(A), v8(src), cosb, ALU.mult), rd=list(skeys) + ["ropec"], wr=["rtA"] + hA)
        for h in range(2):
            sb_ = sinv[:, :, h, :].unsqueeze(1).broadcast_to([128, 8, 2, 16])
            self.op("dve", lambda e, h=h, sb_=sb_: e.tensor_tensor(v5(B)[:, :, :, h, :], v5(src)[:, :, :, 1 - h, :], sb_, ALU.mult),
                    rd=list(skeys) + ["ropes"], wr=["rtB"] + hB)
        self.op("dve", lambda e: e.tensor_tensor(dst, A, B, ALU.add), rd=["rtA", "rtB"], wr=[dkey])

    def rmsn(self, src, skey, nh, normbc, nkey, dst, dkey):
        sq = self.rtA[:, 0:nh * 64]
        ss = self.rstat[:, 0:nh]
        v = lambda ap: ap.rearrange("p (g f) -> p g f", g=nh)
        self.op("act", lambda e: e.activation(sq, src, AF.Square), rd=[skey], wr=["rtA"])
        self.op("dve", lambda e: e.tensor_reduce(ss, v(sq), AX.X, ALU.add), rd=["rtA"], wr=["rstat"])
        self.op("dve", lambda e: e.tensor_scalar(ss, ss, 1.0 / 64, RMS_EPS, ALU.mult, ALU.add), rd=["rstat"], wr=["rstat"])
        self.op("act", lambda e: e.sqrt(ss, ss), rd=["rstat"], wr=["rstat"])
        self.op("dve", lambda e: e.reciprocal(ss, ss), rd=["rstat"], wr=["rstat"])
        self.op("dve", lambda e: e.tensor_tensor(v(dst), v(src), ss.unsqueeze(2).broadcast_to([128, nh, 64]), ALU.mult),
                rd=[skey, "rstat"], wr=[dkey])
        self.op("dve", lambda e: e.tensor_tensor(v(dst), v(dst), normbc.unsqueeze(1).broadcast_to([128, nh, 64]), ALU.mult),
                rd=[dkey, nkey], wr=[dkey])

    def state_out(self, name, src_ps, skey, t, col0, ncols):
        p = 0
        sg, kg = self.stg[:, p, 0:ncols], ("stg", p)
        self.op("act", lambda e: e.activation(sg, src_ps, AF.Copy), rd=[skey], wr=[kg])
        self.S.dma("sp", self.dram[name][t * 128:(t + 1) * 128, col0:col0 + ncols], sg, rd=[kg])

    def attn_layer(self, kind, i):
        L = self.ATT[kind]
        d = self.dram
        nh, dv, nvh = L["nh"], L["dv"], L["nvh"]
        ph = ExitStack()
        sb = lambda n, s, dt: self.sb(n, s, dt, ph)
        self.qT = sb("qT", [128, 8, 512], BF16)
        self.kT = sb("kT", [128, 8, 1280], BF16)
        self.V = sb("Vaug", [128, 10, nvh, dv + 1], BF16)
        self.Otok = sb("Otok", [128, 1, 4 if kind == "na" else 2, D], BF16)
        self.PT = sb("PT", [128, 2, 4 if kind == "na" else 10, 256], BF16)
        if kind != "na":
            self.rtA = sb("rtA", [128, 512], F32)
            self.rtB = sb("rtB", [128, 512], F32)
            self.ropec = sb("ropec_sb", [128, 8, 64], F32)
            self.ropes = sb("ropes_sb", [128, 8, 64], F32)
            self.S.dma("sp", self.ropec[:], d["ropec"].rearrange("p (a b) -> p a b", a=8), wr=["ropec"])
            self.S.dma("sp", self.ropes[:], d["ropes"].rearrange("p (a b) -> p a b", a=8), wr=["ropes"])
        self.rstat = sb("rstat", [128, 16], F32)
        self.qkst = sb("qkst", [128, 3, 512], BF16)
        self.stg = sb("stg", [128, 1, 512], F32)
        self.asm = sb("asm", [128, 2, 8], F32)
        self.stg_i = 0
        self.qk_i = 0
        self.qkq = []
        self.att_i = 0
        if kind == "da":
            self.lamt = self.rtA[:, 0:256]
            self.lamv = sb("lamv", [128, 4], F32)
            self.gsub = sb("gsub", [128, 128], F32)
            self.S.dma("sp", self.lamt, d["da_lambda"].partition_broadcast(128), wr=["rtA"])
            self.S.dma("sp", self.gsub[:], d["da_subln_g"].partition_broadcast(128), wr=["gsub"])
            lam_init = 0.8 - 0.6 * float(np.exp(-0.3 * i))
            self.lam_init = lam_init
            lt_ = self.lamt
            for j in range(2):
                self.op("dve", lambda e, j=j: e.tensor_tensor(lt_[:, j * 128:j * 128 + 64], lt_[:, j * 128:j * 128 + 64],
                                                             lt_[:, j * 128 + 64:j * 128 + 128], ALU.mult), rd=["rtA"], wr=["rtA"])
                self.op("dve", lambda e, j=j: e.tensor_reduce(self.lamv[:, j:j + 1], lt_[:, j * 128:j * 128 + 64], AX.X, ALU.add),
                        rd=["rtA"], wr=["lamv"])
            self.op("act", lambda e: e.activation(self.lamv[:, 0:2], self.lamv[:, 0:2], AF.Exp), rd=["lamv"], wr=["lamv"])
            self.op("dve", lambda e: e.tensor_tensor(self.lamv[:, 2:3], self.lamv[:, 1:2], self.lamv[:, 0:1], ALU.subtract),
                    rd=["lamv"], wr=["lamv"])
            self.op("dve", lambda e: e.tensor_scalar_add(self.lamv[:, 2:3], self.lamv[:, 2:3], -lam_init), rd=["lamv"], wr=["lamv"])
            self.op("dve", lambda e: e.tensor_scalar_mul(self.gsub[:], self.gsub[:], 1.0 - lam_init), rd=["gsub"], wr=["gsub"])
        if kind == "gq":
            ring3 = sb("ring3", [128, 1, 8, 512], BF16)
            self.slots = self.slots[:NSLOT] + [(ring3[:, s_], ("ring3", s_)) for s_ in range(1)]
            self.qn = sb("qn", [128, 64], F32)
            self.kn = sb("kn", [128, 64], F32)
            self.nrm = sb("nrm", [128, 512], F32)
            self.kdup = sb("kdup", [128, 3, 4, 2, 64], BF16)
            self.S.dma("sp", self.qn[:], d["gq_q_norm"].partition_broadcast(128), wr=["qn"])
            self.S.dma("sp", self.kn[:], d["gq_k_norm"].partition_broadcast(128), wr=["kn"])
        if kind == "na":
            self.tab = sb("tab", [128, 2, 16, 2, 64], BF16)
            self.idlo = sb("idlo_sb", [128, 128], BF16)
            self.idhi = sb("idhi_sb", [128, 128], BF16)
            self.neg = sb("neg_sb", [128, 64], BF16)
            self.S.dma("pool", self.idlo[:], d["idlo"], wr=["idlo"])
            self.S.dma("pool", self.idhi[:], d["idhi"], wr=["idhi"])
            self.S.dma("pool", self.neg[:], d["neg"], wr=["neg"])
        self.op("dve", lambda e: e.memset(self.V[:, :, :, dv:dv + 1], 1.0), rd=[], wr=[("V", j) for j in range(10)])
        self.gate_bcast(0)
        wo = d[L["wo"]]
        for pi in range(3):
            sample = pi > 0
            if pi == 0:
                tiles = [0, 1, 2, 3]
                self.qkv_pass(kind, L, tiles, False, ("q", "k", "v", "kv"), 0)
                self.ln_flush()
                self.lazy_flush()
                self.load_lnbc(i, 0)
                slots = [self.wload(wo[:, hf * 512:(hf + 1) * 512], 512, pin=True) for hf in range(2)]
                for lt0 in (0, 2):
                    self.dense_attn(kind, L, lt0, 2, [lt0, lt0 + 1])
            else:
                qh = pi - 1
                if qh == 0:
                    self.qkv_pass(kind, L, list(range(4, 12)), True, ("k", "v", "kv"), 0)
                    self.load_cache(kind, L)
                tiles = list(range(4 + 4 * qh, 8 + 4 * qh))
                self.qkv_pass(kind, L, tiles, True, ("q",), 4 * qh)
                slots = [self.wload(wo[:, hf * 512:(hf + 1) * 512], 512, pin=True) for hf in range(2)]
                if kind == "na":
                    self.na_sample_attn(L, 4 * qh)
                else:
                    self.dense_attn(kind, L, 0, 4, list(range(10)))
            items = []
            for lt, t in enumerate(tiles):
                bs = []
                for hf in range(2):
                    b = self.newbank()
                    self.held.add(b)
                    slot, rk = slots[hf]
                    self.mmg(self.PS[b][:], [(self.qT[:, kc, lt * 128:(lt + 1) * 128], slot[:, kc, :]) for kc in range(8)],
                             rd=[("qT", lt), rk], wr=[("ps", b)])
                    bs.append(b)
                items.append((t, [self.PS[b][:] for b in bs], [("ps", b) for b in bs], 1))
                if len(items) == 2 or lt == len(tiles) - 1:
                    self.ln_tiles(items)
                    for it in items:
                        for kb_ in it[2]:
                            self.held.discard(kb_[1])
                    items = []
            for (_s, k_) in slots:
                self.pinned.discard(k_)
            self.ada_tick(9)
        names = ("qT", "kT", "V", "Otok", "PT", "rtA", "rtB", "rstat", "qkst", "stg", "cst", "asm", "of32", "lamt", "lamv",
                 "gsub", "qn", "kn", "nrm", "kdup", "tab", "idlo", "idhi", "neg", "sqd", "rtA2", "ropec", "ropes", "rtAh", "rtBh", "ring3")
        self.S.retire(lambda k: (k in names) or (isinstance(k, tuple) and k[0] in names))
        self.slots = self.slots[:NSLOT]
        self.ring_i = self.ring_i % NSLOT
        ph.close()

    def qk_to_T(self, src_bf, skey, dstT, c0, lt, dkey, defer=True):
        def run():
            def dst(pv, pk):
                self.op("act", lambda e: e.activation(dstT[:, c0:c0 + 4, lt * 128:(lt + 1) * 128], pv, AF.Copy),
                        rd=[pk], wr=[(dkey, lt)])
            self.transposes(src_bf, 4, [skey], dst, None)
        self.qkq.append(run)
        if len(self.qkq) > (1 if defer else 0):
            self.qkq.pop(0)()

    def qk_flush(self):
        while self.qkq:
            self.qkq.pop(0)()

    def qkv_pass(self, kind, L, tiles, sample, which, rope0):
        self._qkv_pass(kind, L, tiles, sample, which, rope0)
        self.qk_flush()

    def _qkv_pass(self, kind, L, tiles, sample, which, rope0):
        d = self.dram
        w = d[L["w"]]
        dv, nvh = L["dv"], L["nvh"]
        for (cn, col0, ncols) in L["chunks"]:
            if cn not in which:
                continue
            slot, rk = self.wload(w[:, col0:col0 + ncols], ncols)
            for lt, t in enumerate(tiles):
                self.lazy_flush(1)
                b = self.newbank()
                ps, pk = self.PS[b][:], ("ps", b)
                self.mmg(ps, [(self.hT[:, kc, t * 128:(t + 1) * 128], slot[:, kc, :]) for kc in range(8)],
                         rd=[("hT", t), rk], wr=[pk])
                qi = self.qk_i % 3
                self.qk_i += 1
                st, ks = self.qkst[:, qi, :], ("qkst", qi)
                if cn in ("q", "k"):
                    fc0 = (col0 % 1024) // 128
                    if kind == "gq":
                        self.rmsn(ps, pk, 8, self.qn[:], "qn", self.nrm[:], "nrm")
                        if sample:
                            self.rope(self.nrm[:], ["nrm"], rope0 + lt, st, ks)
                        else:
                            self.op("act", lambda e: e.activation(st, self.nrm[:], AF.Copy), rd=["nrm"], wr=[ks])
                    else:
                        if cn == "k" and not sample:
                            self.state_out(f"st_{kind}_k", ps, pk, t, col0 - 1024, 512)
                        if kind == "da" and sample:
                            self.rope(ps, [pk], rope0 + lt, st, ks)
                        else:
                            self.op("act", lambda e: e.activation(st, ps, AF.Copy), rd=[pk], wr=[ks])
                    self.qk_to_T(st, ks, self.qT if cn == "q" else self.kT, fc0, lt, "qT" if cn == "q" else "kT")
                elif cn == "v":
                    h0 = (col0 - 2048) // dv
                    nhc = 512 // dv
                    if not sample:
                        self.state_out(f"st_{kind}_v", ps, pk, t, col0 - 2048, 512)
                    self.op("act", lambda e, h0=h0, nhc=nhc: e.activation(
                        self.V[:, lt, h0:h0 + nhc, 0:dv], ps.rearrange("p (h f) -> p h f", h=nhc), AF.Copy),
                        rd=[pk], wr=[("V", lt)])
                else:
                    kn_, kk = self.nrm[:, 0:256], "nrm"
                    self.rmsn(ps[:, 0:256], pk, 4, self.kn[:], "kn", kn_, kk)
                    if not sample:
                        p = 0
                        sg, kg = self.stg[:, p, 0:256], ("stg", p)
                        self.op("act", lambda e: e.activation(sg, kn_, AF.Copy), rd=[kk], wr=[kg])
                        self.S.dma("sp", d["st_gq_k"][t * 128:(t + 1) * 128, :], sg, rd=[kg])
                        self.state_out("st_gq_v", ps[:, 256:512], pk, t, 0, 256)
                        ksrc, kkeys = kn_, [kk]
                    else:
                        self.op("act", lambda e: e.activation(self.nrm[:, 256:512], self.nrm[:, 0:256], AF.Copy), rd=[kk], wr=[kk])
                        self.rope(self.nrm[:], [kk], rope0 + lt, self.rtA[:], "rtA2")
                        ksrc, kkeys = self.rtA[:, 0:256], ["rtA2", "rtA"]
                    kd = self.kdup[:, qi]
                    for dup in range(2):
                        self.op("act", lambda e, dup=dup: e.activation(
                            kd[:, :, dup, :], ksrc.rearrange("p (h f) -> p h f", h=4), AF.Copy),
                            rd=kkeys, wr=[("kdup", qi)])
                    self.qk_to_T(kd.rearrange("p h u f -> p (h u f)"), ("kdup", qi), self.kT, 0, lt, "kT")
                    self.op("act", lambda e: e.activation(
                        self.V[:, lt, 0:4, 0:64], ps[:, 256:512].rearrange("p (h f) -> p h f", h=4), AF.Copy),
                        rd=[pk], wr=[("V", lt)])

    def load_cache(self, kind, L):
        d = self.dram
        dv, nvh = L["dv"], L["nvh"]
        ck, cv = d[L["ck"]], d[L["cv"]]
        for j in range(2):
            lt = 8 + j
            if kind == "gq":
                qi = self.qk_i % 3
                self.qk_i += 1
                kd = self.kdup[:, qi]
                for dup in range(2):
                    self.S.dma("pool", kd[:, :, dup, :], ck[j * 128:(j + 1) * 128, :].rearrange("p (h f) -> p h f", h=4),
                               wr=[("kdup", qi)])
                self.qk_to_T(kd.rearrange("p h u f -> p (h u f)"), ("kdup", qi), self.kT, 0, lt, "kT")
            else:
                for hf in range(2):
                    qi = self.qk_i % 3
                    self.qk_i += 1
                    self.S.dma("pool", self.qkst[:, qi, :], ck[j * 128:(j + 1) * 128, hf * 512:(hf + 1) * 512], wr=[("qkst", qi)])
                    self.qk_to_T(self.qkst[:, qi, :], ("qkst", qi), self.kT, hf * 4, lt, "kT")
            self.S.dma("pool", self.V[:, lt, :, 0:dv], cv[j * 128:(j + 1) * 128, :].rearrange("p (h f) -> p h f", h=nvh),
                       wr=[("V", lt)])
        self.qk_flush()

    def head_ops(self, kind, h, c):
        if kind == "da":
            return slice(c * 64, c * 64 + 64), h, h, h
        if kind == "na":
            return slice((h % 2) * 64, (h % 2) * 64 + 64), h // 2, h // 2, h
        return slice((h % 2) * 64, (h % 2) * 64 + 64), h // 2, h // 4, h // 4

    def dense_attn(self, kind, L, lt0, nt, ktiles):
        nh, dv = L["nh"], L["dv"]
        ncomp = 2 if kind == "da" else 1
        nk = len(ktiles)
        units = [(h, c) for h in range(nh) for c in range(ncomp)]
        for qb in range(nt // 2):
            q0 = (lt0 + qb * 2) * 128
            qtl = [lt0 + qb * 2, lt0 + qb * 2 + 1]
            oi = 0
            Ot = self.Otok[:, oi]

            def stage_s(ui):
                h, c = units[ui]
                psl, qc, kc_, vh = self.head_ops(kind, h, c)
                pset = ui % 2
                for k0 in range(0, nk, 2):
                    kk = ktiles[k0:k0 + 2]
                    b = self.newbank()
                    for jj, kt in enumerate(kk):
                        self.mmg(self.PS[b][:, jj * 256:(jj + 1) * 256],
                                 [(self.kT[psl, kc_, kt * 128:(kt + 1) * 128], self.qT[psl, qc, q0:q0 + 256])],
                                 rd=[("kT", kt), ("qT", qtl[0]), ("qT", qtl[1])], wr=[("ps", b)])
                    n = len(kk)
                    self.op("act", lambda e, b=b, n=n, k0=k0: e.activation(
                        self.PT[:, pset, k0:k0 + n, :], self.PS[b][:, 0:n * 256].rearrange("p (a q) -> p a q", a=n),
                        AF.Exp, scale=0.125), rd=[("ps", b)], wr=[("PT", pset)])

            def stage_v(ui, accs):
                h, c = units[ui]
                psl, qc, kc_, vh = self.head_ops(kind, h, c)
                pset = ui % 2
                b = self.newbank()
                self.held.add(b)
                for qt in range(2):
                    self.mmg(self.PS[b][:, qt * (dv + 1):(qt + 1) * (dv + 1)],
                             [(self.PT[:, pset, k_i, qt * 128:(qt + 1) * 128], self.V[:, kt, vh, :]) for k_i, kt in enumerate(ktiles)],
                             rd=[("PT", pset)] + [("V", kt) for kt in ktiles], wr=[("ps", b)])
                accs.append(b)
                if len(accs) == ncomp:
                    g_ = self.attn_evac_gen(kind, h, list(accs), Ot, oi, dv)
                    if kind == "da" and pend[0] is None:
                        pend[0] = (g_, list(accs))
                    else:
                        gl, bl = [g_], list(accs)
                        if pend[0] is not None:
                            gl = [pend[0][0], g_]
                            bl += pend[0][1]
                            pend[0] = None
                        self.run_gens(gl)
                        for bb in bl:
                            self.held.discard(bb)
                    accs.clear()

            accs = []
            pend = [None]
            stage_s(0)
            for ui in range(len(units)):
                if ui + 1 < len(units):
                    stage_s(ui + 1)
                stage_v(ui, accs)
            if pend[0] is not None:
                self.run_gens([pend[0][0]])
                for bb in pend[0][1]:
                    self.held.discard(bb)
                pend[0] = None
            self.ada_tick(9)
            for qt in range(2):
                lt = qtl[qt]

                def dst(pv, pk, lt=lt):
                    self.op("act", lambda e: e.activation(self.qT[:, :, lt * 128:(lt + 1) * 128], pv, AF.Copy),
                            rd=[pk], wr=[("qT", lt)])
                self.transposes(Ot[:, qt, :], 8, [("Otok", oi)], dst, None)

    def attn_evac_gen(self, kind, h, accs, Ot, oi, dv):
        ai = self.att_i2 = getattr(self, "att_i2", 0) + 1
        par = ai % 2
        sm = self.asm[:, par]
        ksm = ("asm", par)
        bc = lambda ap, n: ap.unsqueeze(2).broadcast_to([128, 2, n])
        if kind != "da":
            b = accs[0]
            A = self.PS[b][:, 0:2 * (dv + 1)].rearrange("p (q c) -> p q c", c=dv + 1)
            self.op("dve", lambda e: e.reciprocal(sm[:, 0:2], A[:, :, dv]), rd=[("ps", b)], wr=[ksm])
            self.op("dve", lambda e: e.tensor_tensor(Ot[:, 0:2, h * 64:(h + 1) * 64], A[:, :, 0:dv], bc(sm[:, 0:2], dv), ALU.mult),
                    rd=[("ps", b), ksm], wr=[("Otok", oi)])
            return
        b1, b2 = accs
        A1 = self.PS[b1][:, 0:258].rearrange("p (q c) -> p q c", c=129)
        A2 = self.PS[b2][:, 0:258].rearrange("p (q c) -> p q c", c=129)
        of = self.rtA[:, par * 256:(par + 1) * 256].rearrange("p (q c) -> p q c", c=128)
        t2 = self.rtB[:, par * 256:(par + 1) * 256].rearrange("p (q c) -> p q c", c=128)
        ko, kt2 = ("rtAh", par), ("rtBh", par)
        self.op("dve", lambda e: e.reciprocal(sm[:, 0:2], A1[:, :, 128]), rd=[("ps", b1)], wr=[ksm])
        yield
        self.op("dve", lambda e: e.reciprocal(sm[:, 2:4], A2[:, :, 128]), rd=[("ps", b2)], wr=[ksm])
        yield
        self.op("dve", lambda e: e.tensor_scalar_mul(sm[:, 2:4], sm[:, 2:4], self.lamv[:, 2:3]), rd=[ksm, "lamv"], wr=[ksm])
        yield
        self.op("dve", lambda e: e.tensor_tensor(of, A1[:, :, 0:128], bc(sm[:, 0:2], 128), ALU.mult), rd=[("ps", b1), ksm], wr=[ko])
        yield
        self.op("dve", lambda e: e.tensor_tensor(t2, A2[:, :, 0:128], bc(sm[:, 2:4], 128), ALU.mult), rd=[("ps", b2), ksm], wr=[kt2])
        yield
        self.op("dve", lambda e: e.tensor_tensor(of, of, t2, ALU.add), rd=[ko, kt2], wr=[ko])
        yield
        self.op("dve", lambda e: e.tensor_tensor(t2, of, of, ALU.mult), rd=[ko, kt2], wr=[kt2])
        yield
        self.op("dve", lambda e: e.tensor_reduce(sm[:, 4:6], t2, AX.X, ALU.add), rd=[kt2], wr=[ksm])
        yield
        self.op("dve", lambda e: e.tensor_scalar(sm[:, 4:6], sm[:, 4:6], 1.0 / 128, RMS_EPS, ALU.mult, ALU.add), rd=[ksm], wr=[ksm])
        yield
        self.op("act", lambda e: e.sqrt(sm[:, 4:6], sm[:, 4:6]), rd=[ksm], wr=[ksm])
        yield
        self.op("dve", lambda e: e.reciprocal(sm[:, 4:6], sm[:, 4:6]), rd=[ksm], wr=[ksm])
        yield
        self.op("dve", lambda e: e.tensor_tensor(of, of, bc(sm[:, 4:6], 128), ALU.mult), rd=[ko, ksm], wr=[ko])
        yield
        self.op("dve", lambda e: e.tensor_tensor(Ot[:, 0:2, h * 128:(h + 1) * 128], of,
                                                 self.gsub[:].unsqueeze(1).broadcast_to([128, 2, 128]), ALU.mult),
                rd=[ko, "gsub"], wr=[("Otok", oi)])
        yield


    def run_gens(self, gens):
        gens = list(gens)
        while gens:
            for g_ in list(gens):
                try:
                    next(g_)
                except StopIteration:
                    gens.remove(g_)

    def na_sample_attn(self, L, m0):
        PTf = self.PT[:].rearrange("p s a q -> p s (a q)")
        d = self.dram
        units = [(h, m) for h in range(16) for m in range(m0, m0 + 4)]

        def geom(m):
            rows = (2 * m, 2 * m + 1)
            rs = [min(max(r - 4, 0), 8) for r in rows]
            chunks = list(range(min(rs) // 2, (max(rs) + 7) // 2 + 1))
            return rows, rs, chunks + [8, 9]

        def stage_s(ui):
            h, m = units[ui]
            ml = m - m0
            rows, rs, allc = geom(m)
            if ml == 0:
                self.S.dma("pool", self.tab[:, h % 2].rearrange("p b a c -> p (b a c)"), d["na_tab"][:, h * 2048:(h + 1) * 2048],
                           wr=[("tab", h % 2)])
            tabh, tk = self.tab[:, h % 2], ("tab", h % 2)
            psl = slice((h % 2) * 64, (h % 2) * 64 + 64)
            qc = h // 2
            pset = ui % 2
            for bi in range(0, len(allc), 4):
                cc = allc[bi:bi + 4]
                b = self.newbank()
                for jj, c in enumerate(cc):
                    out = self.PS[b][:, jj * 128:(jj + 1) * 128]
                    mms = [(out, self.kT[psl, qc, c * 128:(c + 1) * 128], self.qT[psl, qc, ml * 128:(ml + 1) * 128], [("kT", c), ("qT", ml)])]
                    if c < 8:
                        kb = 2 * c
                        ep = kb - 2 * m + 7
                        mms.append((out, self.ident[:], tabh[:, ep].rearrange("p a c -> p (a c)"), ["ident", tk]))
                        for a in range(2):
                            v0 = rs[a] <= kb < rs[a] + 8
                            v1 = rs[a] <= kb + 1 < rs[a] + 8
                            oa = self.PS[b][:, jj * 128 + a * 64:jj * 128 + (a + 1) * 64]
                            if v0 or v1:
                                if not v0:
                                    mms.append((oa, self.idlo[:], self.neg[:], ["idlo", "neg"]))
                                if not v1:
                                    mms.append((oa, self.idhi[:], self.neg[:], ["idhi", "neg"]))
                            else:
                                mms.append((oa, self.ident[:], self.neg[:], ["ident", "neg"]))
                    n = len(mms)
                    for j, (o_, l_, r_, rd_) in enumerate(mms):
                        self.op("pe", lambda e, o_=o_, l_=l_, r_=r_, j=j, n=n: e.matmul(o_, lhsT=l_, rhs=r_, start=(j == 0), stop=(j == n - 1)),
                                rd=rd_, wr=[("ps", b)], inc=(j == n - 1))
                ncol = len(cc) * 128
                self.op("act", lambda e, b=b, bi=bi, ncol=ncol: e.activation(
                    PTf[:, pset, bi * 128:bi * 128 + ncol], self.PS[b][:, 0:ncol], AF.Exp, scale=0.125),
                    rd=[("ps", b)], wr=[("PT", pset)])

        def stage_v(ui):
            h, m = units[ui]
            ml = m - m0
            rows, rs, allc = geom(m)
            pset = ui % 2
            b = self.newbank()
            self.held.add(b)
            self.mmg(self.PS[b][:, 0:65],
                     [(PTf[:, pset, ci * 128:(ci + 1) * 128], self.V[:, c, h, :]) for ci, c in enumerate(allc)],
                     rd=[("PT", pset)] + [("V", c) for c in allc], wr=[("ps", b)])
            ai = self.att_i2 = getattr(self, "att_i2", 0) + 1
            sm, ksm = self.asm[:, ai % 2], ("asm", ai % 2)
            self.op("dve", lambda e: e.reciprocal(sm[:, 0:1], self.PS[b][:, 64:65]), rd=[("ps", b)], wr=[ksm])
            self.op("act", lambda e: e.activation(self.Otok[:, 0, ml, h * 64:(h + 1) * 64], self.PS[b][:, 0:64], AF.Copy, scale=sm[:, 0:1]),
                    rd=[("ps", b), ksm], wr=[("Otok", ml)])
            self.held.discard(b)

        stage_s(0)
        for ui in range(len(units)):
            if ui + 1 < len(units):
                stage_s(ui + 1)
            stage_v(ui)
            if ui % 16 == 15:
                self.ada_tick(9)
        for ml in range(4):
            def dst(pv, pk, ml=ml):
                self.op("act", lambda e: e.activation(self.qT[:, :, ml * 128:(ml + 1) * 128], pv, AF.Copy),
                        rd=[pk], wr=[("qT", ml)])
            self.transposes(self.Otok[:, 0, ml, :], 8, [("Otok", ml)], dst, None)

    def build(self):
        self.setup()
        self.ada0_it = self.ada_gen(0)
        if self.n_layers >= 4:
            self.hyena_filters()
        while self.ada0_it is not None:
            self.ada0_tick()
        self.first_h()
        kinds = ["da", "na", "gq", "hy"]
        self.ada_it = None
        for i in range(self.n_layers):
            self.cur_layer = i
            self.ada_it = self.ada_gen(i + 1) if i + 1 < self.n_layers else None
            self.ada_steps = 0
            if kinds[i] == "hy":
                self.hyena_layer(i)
            else:
                self.attn_layer(kinds[i], i)
            last = i == self.n_layers - 1
            self.mlp_phase(i, None, last)
            self.ada_tick(12, 12)
        self.S.finish()
        return self.nc


def _host_consts():
    c = {}
    c["ident"] = np.eye(128, dtype=np.float32)
    lo = np.zeros((128, 128), np.float32); lo[np.arange(64), np.arange(64)] = 1
    hi = np.zeros((128, 128), np.float32); hi[np.arange(64, 128), np.arange(64, 128)] = 1
    c["idlo"], c["idhi"] = lo, hi
    c["neg"] = np.full((128, 64), -1e30, np.float32)
    sel = np.zeros((2, 256), np.float32); sel[0, :128] = 1; sel[1, 128:] = 1
    c["sel"] = sel
    pos = np.arange(1024)
    inv = (10000.0 ** (-np.arange(16, dtype=np.float32) / 16)).astype(np.float32)
    ar = ((pos // 64).astype(np.float32)[:, None] * inv).astype(np.float32)
    ac = ((pos % 64).astype(np.float32)[:, None] * inv).astype(np.float32)
    cr, sr, cc, sc = np.cos(ar), np.sin(ar), np.cos(ac), np.sin(ac)
    cosf = np.concatenate([cr, cr, cc, cc], -1).astype(np.float32)
    sinf = np.concatenate([-sr, sr, -sc, sc], -1).astype(np.float32)
    c["ropec"] = np.ascontiguousarray(cosf.reshape(8, 128, 64).transpose(1, 0, 2).reshape(128, 512))
    c["ropes"] = np.ascontiguousarray(sinf.reshape(8, 128, 64).transpose(1, 0, 2).reshape(128, 512))
    for Ln in (256, 1024):
        t = (np.arange(Ln, dtype=np.float32) / np.float32(Ln)).astype(np.float32)
        ang = (np.float32(2.0 * np.pi) * t[:, None] * np.arange(1, 17, dtype=np.float32)).astype(np.float32)
        emb = np.concatenate([t[:, None], np.cos(ang), np.sin(ang)], -1).astype(np.float32)
        c[f"embT{Ln}"] = np.ascontiguousarray(emb.T)
        c[f"tcol{Ln}"] = np.ascontiguousarray((-t).reshape(Ln // 128, 128).T)
        tt = np.arange(Ln, dtype=np.float64)[:, None]
        w = np.pi * (2 * np.arange(Ln, dtype=np.float64)[None, :] + 1) / (2 * Ln)
        Cm, Sm = np.cos(tt * w), np.sin(tt * w)
        bf = ml_dtypes.bfloat16
        c[f"Cm{Ln}"] = np.ascontiguousarray(Cm.astype(np.float32).astype(bf)); c[f"Sm{Ln}"] = np.ascontiguousarray(Sm.astype(np.float32).astype(bf))
        c[f"CmT{Ln}"] = np.ascontiguousarray(Cm.T.astype(np.float32).astype(bf)); c[f"SmT{Ln}"] = np.ascontiguousarray(Sm.T.astype(np.float32).astype(bf))
    return c


def _na_table(rel_bias):
    rb = np.asarray(rel_bias, np.float32)
    i = np.arange(2)[:, None, None, None, None]
    kc = np.arange(64)[None, :, None, None, None]
    ep = np.arange(16)[None, None, :, None, None]
    a = np.arange(2)[None, None, None, :, None]
    qc = np.arange(64)[None, None, None, None, :]
    e = ep - a
    dr = e + i - 7
    cs = np.clip(qc - 8, 0, 48)
    shp = (2, 64, 16, 2, 64)
    ok = np.broadcast_to((e >= 0) & (e <= 14) & (np.abs(dr) <= 7) & (kc >= cs) & (kc < cs + 16), shp)
    ri = np.broadcast_to(np.clip(dr + 7, 0, 14), shp)
    ci = np.broadcast_to(np.clip(kc - qc, -15, 15) + 15, shp)
    tab = np.empty((2, 64, 16, 16, 2, 64), np.float32)
    for h in range(16):
        tab[:, :, h] = np.where(ok, rb[h][ri, ci], np.float32(-1e30))
    return np.ascontiguousarray(tab.reshape(128, 16 * 2048))


_NC_CACHE = {}


def kernel(**inp):
    n_layers = int(inp.pop("_n_layers", 4))
    f = lambda a: np.ascontiguousarray(np.asarray(a, dtype=np.float32))
    if n_layers not in _NC_CACHE:
        kb = KB(n_layers)
        _NC_CACHE[n_layers] = kb.build()
    nc = _NC_CACHE[n_layers]
    consts = _host_consts()
    shared = dict(consts)
    shared["ada_w"] = f(inp["ada_w"])
    ab = f(inp["ada_b"])
    shared["ada_b"] = ab
    shared["adab_col"] = np.ascontiguousarray(np.repeat(ab.reshape(4, 48, 128).transpose(0, 2, 1)[:, :, :, None], 2, axis=3).reshape(4, 128, 96))
    shared["ln_g"], shared["ln_b"] = f(inp["ln_g"]), f(inp["ln_b"])
    shared["lng_col"] = np.ascontiguousarray(f(inp["ln_g"]).reshape(4, 2, 8, 128).transpose(3, 0, 1, 2).reshape(128, 64))
    shared["lnb_col"] = np.ascontiguousarray(f(inp["ln_b"]).reshape(4, 2, 8, 128).transpose(3, 0, 1, 2).reshape(128, 64))
    shared["mlp_w1"], shared["mlp_w2"] = f(inp["mlp_w1"]), f(inp["mlp_w2"])
    shared["da_w_qkv"], shared["da_w_o"] = f(inp["da_w_qkv"])[0], f(inp["da_w_o"])[0]
    shared["da_lambda"] = f(inp["da_lambda"]).reshape(1, 256)
    shared["da_subln_g"] = f(inp["da_subln_g"]).reshape(1, 128)
    shared["na_w_qkv"], shared["na_w_o"] = f(inp["na_w_qkv"])[0], f(inp["na_w_o"])[0]
    shared["na_tab"] = _na_table(f(inp["na_rel_bias"])[0])
    shared["gq_w_qkv"], shared["gq_w_o"] = f(inp["gq_w_qkv"])[0], f(inp["gq_w_o"])[0]
    shared["gq_q_norm"], shared["gq_k_norm"] = f(inp["gq_q_norm"]).reshape(1, 64), f(inp["gq_k_norm"]).reshape(1, 64)
    shared["hy_w_in"], shared["hy_w_o"] = f(inp["hy_w_in"])[0], f(inp["hy_w_o"])[0]
    shared["hy_sw_col"] = np.ascontiguousarray(f(inp["hy_short_w"])[0].reshape(3, 24, 128).transpose(2, 0, 1).reshape(128, 72))
    shared["hy_sb_col"] = np.ascontiguousarray(f(inp["hy_short_b"])[0].reshape(24, 128).T)
    shared["hy_fb_col"] = np.ascontiguousarray(f(inp["hy_filter_bias"])[0].reshape(2, 8, 128).transpose(2, 0, 1).reshape(128, 16))
    shared["hy_w1"], shared["hy_w2"], shared["hy_w3"] = f(inp["hy_ffn_w1"])[0], f(inp["hy_ffn_w2"])[0], f(inp["hy_ffn_w3"])[0]
    shared["hy_cols"] = np.ascontiguousarray(np.stack([f(inp["hy_ffn_b1"])[0], f(inp["hy_ffn_b2"])[0], f(inp["hy_ffn_freq"])[0]], -1))
    shared["hy_log_decay"] = f(inp["hy_log_decay"]).reshape(1, 4096)
    xp, xs, c, cctx = f(inp["x_prompt"]), f(inp["x_sample"]), f(inp["c"]), f(inp["c_ctx"])
    in_maps = []
    for b in range(8):
        m = dict(shared)
        m["xin"] = np.ascontiguousarray(np.concatenate([xp[2 * b].reshape(256, D), xp[2 * b + 1].reshape(256, D), xs[b]], 0))
        cv = np.stack([cctx.reshape(8, 128), c[b].reshape(8, 128)], -1)
        m["cvec"] = np.ascontiguousarray(cv.transpose(1, 0, 2).reshape(128, 16))
        m["ck_da"], m["cv_da"] = f(inp["cache_da_k"][b, 0]).reshape(256, D), f(inp["cache_da_v"][b, 0]).reshape(256, D)
        m["ck_na"], m["cv_na"] = f(inp["cache_na_k"][b, 0]).reshape(256, D), f(inp["cache_na_v"][b, 0]).reshape(256, D)
        m["ck_gq"], m["cv_gq"] = f(inp["cache_gq_k"][b, 0]).reshape(256, 256), f(inp["cache_gq_v"][b, 0]).reshape(256, 256)
        in_maps.append(m)
    res = run_bass_kernel_spmd(nc, in_maps, core_ids=list(range(8)))
    R = res.results
    y = np.stack([r["y"] for r in R])
    y_p = y[:, :512].reshape(16, 256, D)
    y_s = y[:, 512:].reshape(8, 1024, D)

    def st(name, H, dh):
        a = np.stack([r[name] for r in R])
        return np.ascontiguousarray(a.reshape(16, 1, 256, H, dh))
    return (np.ascontiguousarray(y_p), np.ascontiguousarray(y_s),
            st("st_da_k", 8, 128), st("st_da_v", 8, 128), st("st_na_k", 16, 64), st("st_na_v", 16, 64),
            st("st_gq_k", 4, 64), st("st_gq_v", 4, 64))


TWO_PI = float(2 * np.pi)


def _hy_filters(self):
    d = self.dram
    S = self.S
    ph = ExitStack()
    sb = lambda n, s_, dt: self.sb(n, s_, dt, ph)
    embT = sb("embT", [33, 1024], F32)
    w1s = sb("hw1", [33, 64], F32)
    w2s = sb("hw2", [64, 64], F32)
    w3s = sb("hw3", [64, 2, 512], BF16)
    cols = sb("hcols", [64, 4], F32)
    fb = sb("hfb", [64, 2], F32)
    negpi = sb("negpi", [64, 1], F32)
    h1T = sb("h1T", [64, 1024], F32)
    h2T = sb("h2T", [64, 1024], F32)
    ktmp = sb("ktmp", [64, 512], F32)
    ktmp2 = sb("ktmp2", [64, 512], F32)
    hpi = sb("hpi", [64, 1], F32)
    h2b = sb("h2b", [64, 1024], BF16)
    ldec = sb("ldec", [128, 2, 512], F32)
    win = sb("win", [128, 2, 512], F32)
    FB = sb("FB", [128, 2, 512], F32)
    ee = sb("ee", [128, 2, 8, 512], BF16)
    dd = sb("dd", [128, 2, 8, 512], BF16)
    ringf = sb("ringf", [128, 1, 8, 512], BF16)
    self.slots = self.slots[:NSLOT] + [(ringf[:, 0], ("ringf", 0))]
    hst = sb("hst", [128, 2, 512], BF16)
    tcol = sb("tcol", [128, 8], F32)
    S.dma("sp", w1s[:], d["hy_w1"], wr=["hw1"])
    S.dma("sp", w2s[:], d["hy_w2"], wr=["hw2"])
    S.dma("sp", cols[:, 0:3], d["hy_cols"], wr=["hcols"])
    self.op("dve", lambda e: e.memset(negpi[:], -float(np.pi)), wr=["negpi"])
    self.op("dve", lambda e: e.memset(hpi[:], float(np.pi / 2)), wr=["negpi"])
    for j in range(2):
        self.op("dve", lambda e, j=j: e.tensor_tensor(fb[:, j:j + 1], cols[:, j:j + 1], cols[:, 2:3], ALU.mult),
                rd=["hcols"], wr=["hfb"])
    self.hsi = 0
    for Ln in (256, 1024):
        nt = Ln // 128
        S.dma("sp", embT[:, 0:Ln], d[f"embT{Ln}"], wr=["embT"])
        S.dma("sp", tcol[:, 0:nt], d[f"tcol{Ln}"], wr=["tcol"])
        for (wsrc, wk, src, sk, dst, dk, j) in ((w1s, "hw1", embT, "embT", h1T, "h1T", 0), (w2s, "hw2", h1T, "h1T", h2T, "h2T", 1)):
            for n0 in range(0, Ln, 512):
                n = min(512, Ln - n0)
                b = self.newbank()
                kdim = 33 if j == 0 else 64
                self.mmg(self.PS[b][0:64, 0:n], [(wsrc[0:kdim, :], src[0:kdim, n0:n0 + n])], rd=[wk, sk], wr=[("ps", b)])
                dv_ = dst[:, n0:n0 + n]
                self.op("act", lambda e, b=b, n=n, dv_=dv_, j=j: e.activation(dv_, self.PS[b][0:64, 0:n], AF.Identity,
                                                                          bias=fb[:, j:j + 1], scale=cols[:, 2:3]),
                        rd=[("ps", b), "hfb", "hcols"], wr=[dk])
                kt = ktmp[:, 0:n]
                k2 = ktmp2[:, 0:n]
                self.op("act", lambda e, dv_=dv_, kt=kt: e.activation(kt, dv_, AF.Sin, scale=0.125), rd=[dk], wr=["ktmp"])
                self.op("dve", lambda e, kt=kt: e.tensor_tensor(kt, kt, kt, ALU.mult), rd=["ktmp"], wr=["ktmp"])
                self.op("dve", lambda e, kt=kt: e.tensor_scalar(kt, kt, -2.0, 1.0, ALU.mult, ALU.add), rd=["ktmp"], wr=["ktmp"])
                self.op("act", lambda e, dv_=dv_: e.activation(dv_, dv_, AF.Sin, scale=0.25), rd=[dk], wr=[dk])
                self.op("dve", lambda e, dv_=dv_, k2=k2: e.tensor_tensor(k2, dv_, dv_, ALU.mult), rd=[dk], wr=["ktmp2"])
                self.op("dve", lambda e, k2=k2: e.tensor_scalar(k2, k2, -2.0, 1.0, ALU.mult, ALU.add), rd=["ktmp2"], wr=["ktmp2"])
                self.op("dve", lambda e, dv_=dv_, kt=kt: e.tensor_tensor(kt, dv_, kt, ALU.mult), rd=[dk, "ktmp"], wr=["ktmp"])
                self.op("dve", lambda e, dv_=dv_, kt=kt, k2=k2: e.scalar_tensor_tensor(dv_, kt, 4.0, k2, ALU.mult, ALU.mult),
                        rd=["ktmp", "ktmp2"], wr=[dk])
            if j == 1:
                self.op("act", lambda e: e.activation(h2b[:, 0:Ln], h2T[:, 0:Ln], AF.Copy), rd=["h2T"], wr=["h2b"])
            self.ada0_tick()
        def gen_e(dblk, o):
            cF = o * 2048 + dblk * 512
            cB = cF + 1024
            for j, c0 in enumerate((cF, cB)):
                S.dma("pool", w3s[:, j, :], d["hy_w3"][:, c0:c0 + 512], wr=[("hw3", j)])
                S.dma("sp", ldec[:, j, :], d["hy_log_decay"][:, c0:c0 + 512].partition_broadcast(128), wr=[("ldec", j)])
                self.op("act", lambda e, j=j: e.activation(ldec[:, j, :], ldec[:, j, :], AF.Exp), rd=[("ldec", j)], wr=[("ldec", j)])
            for tch in range(nt):
                for j in range(2):
                    b = self.newbank()
                    self.mmg(self.PS[b][:], [(h2b[:, tch * 128:(tch + 1) * 128], w3s[:, j, :])], rd=["h2b", ("hw3", j)], wr=[("ps", b)])
                    self.op("act", lambda e, j=j, tch=tch: e.activation(win[:, j, :], ldec[:, j, :], AF.Exp, scale=tcol[:, tch:tch + 1]),
                            rd=[("ldec", j), "tcol"], wr=[("win", j)])
                    self.op("dve", lambda e, j=j, b=b: e.tensor_tensor(FB[:, j, :], self.PS[b][:], win[:, j, :], ALU.mult),
                            rd=[("ps", b), ("win", j)], wr=[("FB", j)])
                if tch == 0:
                    self.op("dve", lambda e: e.memset(FB[0:1, 1, :], 0.0), rd=[("FB", 1)], wr=[("FB", 1)])
                self.op("dve", lambda e, tch=tch, o=o: e.tensor_tensor(ee[:, o, tch, :], FB[:, 0, :], FB[:, 1, :], ALU.add),
                        rd=[("FB", 0), ("FB", 1)], wr=[("ee", o, tch)])
                self.op("dve", lambda e, tch=tch, o=o: e.tensor_tensor(dd[:, o, tch, :], FB[:, 1, :], FB[:, 0, :], ALU.subtract),
                        rd=[("FB", 0), ("FB", 1)], wr=[("dd", o, tch)])
                yield
            self.ada0_tick()

        def gen_H(dblk, o):
            fbw = min(512, Ln)
            for fb0 in range(0, Ln, fbw):
                sl = {}
                for mi, mname in enumerate(("Cm", "Sm")):
                    sl[mi] = self.wload(d[f"{mname}{Ln}"][:, fb0:fb0 + fbw], fbw, nk=nt)
                for fl in range(fbw // 128):
                    fch = fb0 // 128 + fl
                    for mi, src, skn in ((0, ee, "ee"), (1, dd, "dd")):
                        slot, rk = sl[mi]
                        b = self.newbank()
                        self.mmg(self.PS[b][:], [(slot[:, tch, fl * 128:(fl + 1) * 128], src[:, o, tch, :]) for tch in range(nt)],
                                 rd=[rk] + [(skn, o, tch) for tch in range(nt)], wr=[("ps", b)])
                        hp = self.hsi % 2
                        self.hsi += 1
                        self.op("act", lambda e, b=b, hp=hp: e.activation(hst[:, hp, :], self.PS[b][:], AF.Copy, scale=1.0 / Ln),
                                rd=[("ps", b)], wr=[("hst", hp)])
                        S.dma("sp", d[f"H{Ln}"][o, mi, fch * 128:(fch + 1) * 128, dblk * 512:(dblk + 1) * 512], hst[:, hp, :],
                              rd=[("hst", hp)], wr=[("Hd", Ln, o, mi, fch, dblk)])
                    yield
            self.ada0_tick()

        def run(*gens):
            gens = list(gens)
            while gens:
                for g_ in list(gens):
                    try:
                        next(g_)
                    except StopIteration:
                        gens.remove(g_)

        order = [(dblk, o) for dblk in range(2) for o in range(2)]
        run(gen_e(*order[0]))
        for qi_ in range(len(order)):
            if qi_ + 1 < len(order):
                run(gen_H(*order[qi_]), gen_e(*order[qi_ + 1]))
            else:
                run(gen_H(*order[qi_]))
    names = ("embT", "hw1", "hw2", "hw3", "hcols", "hfb", "negpi", "h1T", "h2T", "ldec", "win", "FB", "ee", "dd", "hst", "tcol", "ktmp", "ringf", "ktmp2", "hpi", "h2b")
    S.retire(lambda k: (k in names) or (isinstance(k, tuple) and k[0] in names))
    self.slots = self.slots[:NSLOT]
    self.ring_i = self.ring_i % NSLOT
    ph.close()


def _hy_layer(self, i):
    d = self.dram
    S = self.S
    ph = ExitStack()
    sb = lambda n, s_, dt: self.sb(n, s_, dt, ph)
    zT = sb("zT", [128, 4, 1024], BF16)
    gT = sb("gT", [128, 4, 1024], BF16)
    ztok = sb("ztok", [128, 8, 512], BF16)
    Yre = sb("Yre", [128, 8, 512], BF16)
    Yng = sb("Yng", [128, 8, 512], BF16)
    z2T = sb("z2T", [128, 8, 1024], BF16)
    v32 = sb("v32", [128, 2, 1024], F32)
    Ht = sb("Ht", [128, 2, 2, 512], BF16)
    t32 = sb("t32", [128, 4, 512], F32)
    swc = sb("swc", [128, 72], F32)
    sbc = sb("sbc", [128, 24], F32)
    fbc = sb("fbc", [128, 16], F32)
    S.dma("sp", swc[:], d["hy_sw_col"], wr=["swc"])
    S.dma("sp", sbc[:], d["hy_sb_col"], wr=["sbc"])
    S.dma("sp", fbc[:], d["hy_fb_col"], wr=["fbc"])
    self.lazy_flush()
    self.load_lnbc(i, 0)
    self.gate_bcast(0)
    w_in = d["hy_w_in"]
    hti = 0
    tti = 0
    self.hv_i = 0
    for (t0, ntile, Ln) in ((0, 2, 256), (2, 2, 256), (4, 8, 1024)):
        if t0 == 2:
            self.ln_flush()
        tok0 = t0 * 128
        nt = Ln // 128
        nb = max(1, Ln // 512)
        bw = min(512, Ln)
        hk = [("hT", t) for t in range(t0, t0 + ntile)]

        def inproj(part, dblk, dstT, dname):
            slot, rk = self.wload(w_in[:, part * 1024 + dblk * 512: part * 1024 + (dblk + 1) * 512], 512)
            pend = []

            def fin(item):
                ch, pp, pkeys, vv, kv, fcg, hb = item
                w2_ = swc[:, 48 + fcg:48 + fcg + 1]
                self.op("dve", lambda e: e.scalar_tensor_tensor(dstT[:, ch, 0:Ln - 1], pp[:, 1:Ln], w2_, vv[:, 0:Ln - 1], ALU.mult, ALU.add),
                        rd=pkeys + [kv, "swc"], wr=[(dname, ch)])
                self.op("act", lambda e: e.activation(dstT[:, ch, Ln - 1:Ln], vv[:, Ln - 1:Ln], AF.Copy), rd=[kv], wr=[(dname, ch)])
                for b_ in hb:
                    self.held.discard(b_)

            for ch in range(4):
                fcg = part * 8 + dblk * 4 + ch
                if nb == 2:
                    b0, pp = self.newpair()
                    hb = [b0, b0 + 1]
                else:
                    b0 = self.newbank()
                    pp = self.PS[b0]
                    hb = [b0]
                for b_ in hb:
                    self.held.add(b_)
                pkeys = [("ps", b_) for b_ in hb]
                for tb in range(nb):
                    self.mmg(pp[:, tb * bw:(tb + 1) * bw], [(slot[:, kc, ch * 128:(ch + 1) * 128], self.hT[:, kc, tok0 + tb * bw: tok0 + (tb + 1) * bw])
                                                             for kc in range(8)], rd=hk + [rk], wr=pkeys)
                vp = self.hv_i % 2
                self.hv_i += 1
                vv, kv = v32[:, vp, :], ("v32", vp)
                w0 = swc[:, fcg:fcg + 1]
                w1_ = swc[:, 24 + fcg:24 + fcg + 1]
                self.op("act", lambda e, pp=pp, vv=vv, w1_=w1_, fcg=fcg: e.activation(vv[:, 0:Ln], pp[:, 0:Ln], AF.Identity, bias=sbc[:, fcg:fcg + 1], scale=w1_),
                        rd=pkeys + ["swc", "sbc"], wr=[kv])
                self.op("dve", lambda e, pp=pp, vv=vv, w0=w0: e.scalar_tensor_tensor(vv[:, 1:Ln], pp[:, 0:Ln - 1], w0, vv[:, 1:Ln], ALU.mult, ALU.add),
                        rd=pkeys + [kv, "swc"], wr=[kv])
                pend.append((ch, pp, pkeys, vv, kv, fcg, hb))
                if len(pend) > 1:
                    fin(pend.pop(0))
            while pend:
                fin(pend.pop(0))

        for dblk in range(2):
            self.mark(f"hy L{Ln} t0={t0} dblk{dblk} inproj z")
            inproj(0, dblk, zT, "zT")
            for o in range(2):
                self.mark(f"hy L{Ln} t0={t0} dblk{dblk} o{o} inproj g")
                inproj(1 + o, dblk, gT, "gT")
                self.mark(f"hy L{Ln} t0={t0} dblk{dblk} o{o} ztok+fwd")
                for tt in range(nt):
                    def dst(pv, pk, tt=tt):
                        self.op("act", lambda e: e.activation(ztok[:, tt, :].rearrange("p (c f) -> p c f", c=4), pv, AF.Copy),
                                rd=[pk], wr=[("ztok", tt)])
                    b = self.newbank()
                    pb = self.PSB[b]
                    for ch in range(4):
                        self.op("pe", lambda e, ch=ch, tt=tt: e.transpose(pb[:, ch * 128:(ch + 1) * 128], zT[:, ch, tt * 128:(tt + 1) * 128], self.ident[:]),
                                rd=[("zT", ch), "ident"], wr=[("ps", b)], inc=(ch == 3))
                    dst(pb[:, 0:512].rearrange("p (c t) -> p c t", c=4), ("ps", b))
                for fb0 in range(0, Ln, bw):
                    sl = [self.wload(d[f"{mn}{Ln}"][:, fb0:fb0 + bw], bw, nk=nt) for mn in ("Cm", "Sm")]
                    for fl in range(bw // 128):
                        fch = fb0 // 128 + fl
                        hp = hti % 2
                        hti += 1
                        for mi in range(2):
                            S.dma("sp", Ht[:, hp, mi, :], d[f"H{Ln}"][o, mi, fch * 128:(fch + 1) * 128, dblk * 512:(dblk + 1) * 512],
                                  rd=[("Hd", Ln, o, mi, fch, dblk)], wr=[("Ht", hp, mi)])
                        bz = []
                        for mi in range(2):
                            slot, rk = sl[mi]
                            b = self.newbank()
                            self.held.add(b)
                            self.mmg(self.PS[b][:], [(slot[:, tch, fl * 128:(fl + 1) * 128], ztok[:, tch, :]) for tch in range(nt)],
                                     rd=[rk] + [("ztok", tch) for tch in range(nt)], wr=[("ps", b)])
                            bz.append(b)
                        Zc, Zs = self.PS[bz[0]], self.PS[bz[1]]
                        kz = [("ps", bz[0]), ("ps", bz[1])]
                        Hc, Hs_ = Ht[:, hp, 0, :], Ht[:, hp, 1, :]
                        prods = ((0, Zc, kz[0], Hc, 0), (1, Zs, kz[1], Hs_, 1), (2, Zs, kz[1], Hc, 0), (3, Zc, kz[0], Hs_, 1))
                        for (pi_, Z_, kz_, H_, hm) in prods:
                            self.op("dve", lambda e, pi_=pi_, Z_=Z_, H_=H_: e.tensor_tensor(t32[:, pi_, :], Z_, H_, ALU.mult),
                                    rd=[kz_, ("Ht", hp, hm)], wr=[("t32", pi_)])
                        self.op("dve", lambda e, fch=fch: e.tensor_tensor(Yre[:, fch, :], t32[:, 0, :], t32[:, 1, :], ALU.add),
                                rd=[("t32", 0), ("t32", 1)], wr=[("Yre", fch)])
                        self.op("dve", lambda e, fch=fch: e.tensor_tensor(Yng[:, fch, :], t32[:, 2, :], t32[:, 3, :], ALU.subtract),
                                rd=[("t32", 2), ("t32", 3)], wr=[("Yng", fch)])
                        for b in bz:
                            self.held.discard(b)
                self.mark(f"hy L{Ln} t0={t0} dblk{dblk} o{o} inverse")
                for tb in range(nb):
                    sl = [self.wload(d[f"{mn}{Ln}"][:, tb * bw:(tb + 1) * bw], bw, nk=nt) for mn in ("CmT", "SmT")]
                    for ch in range(4):
                        b = self.newbank()
                        pairs = []
                        for mi, Y in ((0, Yre), (1, Yng)):
                            slot, rk = sl[mi]
                            pairs += [(Y[:, fch, ch * 128:(ch + 1) * 128], slot[:, fch, 0:bw]) for fch in range(nt)]
                        self.mmg(self.PS[b][:, 0:bw], pairs, rd=[sl[0][1], sl[1][1]] + [("Yre", f_) for f_ in range(nt)] + [("Yng", f_) for f_ in range(nt)],
                                 wr=[("ps", b)])
                        tp = tti % 4
                        tti += 1
                        ta, ka = t32[:, tp, 0:bw], ("t32", tp)
                        fcol = fbc[:, o * 8 + dblk * 4 + ch: o * 8 + dblk * 4 + ch + 1]
                        zsl = zT[:, ch, tb * bw:(tb + 1) * bw]
                        self.op("dve", lambda e, b=b, ta=ta, zsl=zsl, fcol=fcol: e.scalar_tensor_tensor(ta, zsl, fcol, self.PS[b][:, 0:bw], ALU.mult, ALU.add),
                                rd=[("zT", ch), "fbc", ("ps", b)], wr=[ka])
                        if o == 0:
                            odst, okey = zsl, ("zT", ch)
                        else:
                            odst, okey = z2T[:, dblk * 4 + ch, tb * bw:(tb + 1) * bw], ("z2T", dblk * 4 + ch)
                        self.op("dve", lambda e, ta=ta, odst=odst, ch=ch, tb=tb: e.tensor_tensor(odst, ta, gT[:, ch, tb * bw:(tb + 1) * bw], ALU.mult),
                                rd=[ka, ("gT", ch)], wr=[okey])
        self.mark(f"hy L{Ln} t0={t0} wo+ln")
        wo = d["hy_w_o"]
        slots = [self.wload(wo[:, hf * 512:(hf + 1) * 512], 512) for hf in range(2)]
        items = []
        for lt in range(ntile):
            t = t0 + lt
            bs = []
            for hf in range(2):
                b = self.newbank()
                self.held.add(b)
                slot, rk = slots[hf]
                self.mmg(self.PS[b][:], [(z2T[:, kc, lt * 128:(lt + 1) * 128], slot[:, kc, :]) for kc in range(8)],
                         rd=[("z2T", kc) for kc in range(8)] + [rk], wr=[("ps", b)])
                bs.append(b)
            items.append((t, [self.PS[b][:] for b in bs], [("ps", b) for b in bs], 1))
            if len(items) == 2 or lt == ntile - 1:
                self.ln_tiles(items)
                for it in items:
                    for kb_ in it[2]:
                        self.held.discard(kb_[1])
                items = []
    names = ("zT", "gT", "ztok", "Yre", "Yng", "z2T", "v32", "Ht", "t32", "swc", "sbc", "fbc")
    S.retire(lambda k: (k in names) or (isinstance(k, tuple) and k[0] in names))
    ph.close()


KB.hyena_filters = _hy_filters
KB.hyena_layer = _hy_layer
```

```python
import numpy as np
from contextlib import ExitStack
import concourse.bass as bass
import concourse.mybir as mybir
from concourse.bass_utils import run_bass_kernel_spmd

F32 = mybir.dt.float32
BF16 = mybir.dt.bfloat16
AF = mybir.ActivationFunctionType
ALU = mybir.AluOpType
AX = mybir.AxisListType

SEM_LIMIT = 30000
NRING = 8


class _Eng:
    def __init__(self, kb, name, handle):
        self.kb, self.name, self.h = kb, name, handle
        self.cnt = 0
        self.seen = {}
        self.cur = None
        self.new_sem()

    def new_sem(self):
        self.cur = self.kb.add_sem(self.name, self.name)
        self.cnt = 0


class Sched:
    def __init__(self, nc, es):
        self.nc, self.es = nc, es
        self.allsems, self.owner = [], []
        self.E = {}
        for name, h in (("pe", nc.tensor), ("act", nc.scalar), ("dve", nc.vector),
                        ("pool", nc.gpsimd), ("sp", nc.sync)):
            self.E[name] = _Eng(self, name, h)
        self.rings = {}
        for q in ("pool", "sp"):
            self.rings[q] = dict(n=0, sems=[self.add_sem("ring_" + q, "dma") for _ in range(NRING)])
        self.lastw, self.readers = {}, {}
        self.fence = {}
        self.ninstr = 0

    def add_sem(self, name, owner):
        s = self.es.enter_context(self.nc.semaphore(f"s_{name}_{len(self.allsems)}"))
        self.allsems.append(s)
        self.owner.append(owner)
        return len(self.allsems) - 1

    def _wait(self, E, tok):
        si, v = tok
        if E.seen.get(si, 0) >= v:
            return
        E.h.wait_ge(self.allsems[si], v)
        E.seen[si] = v

    def _deps(self, rd, wr, dma=False):
        toks = {}

        def add(t):
            if toks.get(t[0], 0) < t[1]:
                toks[t[0]] = t[1]
        for k in rd:
            w = self.lastw.get(k)
            if w is not None:
                add(w)
        for k in wr:
            w = self.lastw.get(k)
            fresh = w is None and k not in self.readers
            if w is not None:
                add(w)
            for si, v in self.readers.get(k, {}).items():
                add((si, v))
            if fresh and dma:
                for si, v in self.fence.items():
                    add((si, v))
        return toks

    def _record(self, tok, rd, wr):
        for k in rd:
            d = self.readers.setdefault(k, {})
            if d.get(tok[0], 0) < tok[1]:
                d[tok[0]] = tok[1]
        for k in wr:
            self.lastw[k] = tok
            self.readers[k] = {}

    def op(self, eng, fn, rd=(), wr=(), inc=True):
        E = self.E[eng]
        for si, v in self._deps(rd, wr).items():
            if eng == "pe" and self.owner[si] == "pe":
                continue
            self._wait(E, (si, v))
        ins = fn(E.h)
        self.ninstr += 1
        if inc:
            E.cnt += 1
            ins.then_inc(self.allsems[E.cur], 1)
            tok = (E.cur, E.cnt)
            if E.cnt >= SEM_LIMIT:
                E.new_sem()
        else:
            tok = (E.cur, E.cnt + 1)
        self._record(tok, rd, wr)
        return ins

    def dma(self, q, out, in_, rd=(), wr=()):
        E = self.E[q]
        ring = self.rings[q]
        n = ring["n"]
        slot = n % NRING
        for si, v in self._deps(rd, wr, dma=True).items():
            self._wait(E, (si, v))
        if n >= NRING:
            self._wait(E, (ring["sems"][slot], 16 * (n // NRING)))
        E.h.dma_start(out=out, in_=in_).then_inc(self.allsems[ring["sems"][slot]], 16)
        self.ninstr += 1
        tok = (ring["sems"][slot], 16 * (n // NRING + 1))
        ring["n"] = n + 1
        self._record(tok, rd, wr)

    def retire(self, pred):
        toks = {}
        keys = [k for k in set(self.lastw) | set(self.readers) if pred(k)]
        for k in keys:
            w = self.lastw.get(k)
            if w is not None and toks.get(w[0], 0) < w[1]:
                toks[w[0]] = w[1]
            for si, v in self.readers.get(k, {}).items():
                if toks.get(si, 0) < v:
                    toks[si] = v
            self.lastw.pop(k, None)
            self.readers.pop(k, None)
        for name in ("pe", "act", "dve"):
            E = self.E[name]
            for si, v in toks.items():
                if name == "pe" and self.owner[si] == "pe":
                    continue
                self._wait(E, (si, v))
        for si, v in toks.items():
            if self.fence.get(si, 0) < v:
                self.fence[si] = v

    def finish(self):
        E = self.E["sp"]
        for q in ("pool", "sp"):
            ring = self.rings[q]
            n = ring["n"]
            for slot in range(NRING):
                cnt = (n - slot + NRING - 1) // NRING if n > slot else 0
                if cnt > 0:
                    self._wait(E, (ring["sems"][slot], 16 * cnt))
        for name in ("pe", "act", "dve", "pool"):
            e = self.E[name]
            if e.cnt > 0:
                self._wait(E, (e.cur, e.cnt))


D = 1024
NT = 12
T = NT * 128
DFF = 4096
DEPTH = 4
ALPHA = (2 * DEPTH) ** 0.25
LN_EPS = 1e-5 / (ALPHA * ALPHA)
RMS_EPS = 1e-6
NSLOT = 3
SEQS = [(0, 2, "p"), (2, 2, "p"), (4, 8, "s")]


class KB:
    def __init__(self, n_layers=4):
        self.n_layers = n_layers
        self.nc = nc = bass.Bass("TRN2", target_bir_lowering=False)
        self.es = ExitStack()
        self.S = Sched(nc, self.es)
        self.bank_i = 0
        self.ring_i = 0
        self.dram = {}

    def din(self, name, shape):
        self.dram[name] = self.nc.dram_tensor(name, list(shape), F32, kind="ExternalInput").ap()
        return self.dram[name]

    def dout(self, name, shape):
        self.dram[name] = self.nc.dram_tensor(name, list(shape), F32, kind="ExternalOutput").ap()
        return self.dram[name]

    def sb(self, name, shape, dt, es=None):
        self.sb_i = getattr(self, "sb_i", 0) + 1
        return (es or self.es).enter_context(self.nc.sbuf_tensor(f"{name}_{self.sb_i}", list(shape), dt))

    def newbank(self):
        b = self.bank_i % 7
        self.bank_i += 1
        return b

    def op(self, eng, fn, rd=(), wr=(), inc=True):
        if eng == "pe":
            self.npe = getattr(self, "npe", 0) + 1
        return self.S.op(eng, fn, rd, wr, inc)

    def mark(self, name):
        if not hasattr(self, "marks"):
            self.marks = []
        self.marks.append((name, getattr(self, "npe", 0)))

    def mm(self, out, pairs, rd, wr):
        n = len(pairs)
        for j, (l, r) in enumerate(pairs):
            self.op("pe", lambda e, l=l, r=r, j=j: e.matmul(out, lhsT=l, rhs=r, start=(j == 0), stop=(j == n - 1)),
                    rd=rd, wr=wr, inc=(j == n - 1))

    def wload(self, src, ncols, nk=8, pin=False):
        while True:
            self.ring_i = (self.ring_i + 1) % len(self.slots)
            slot, key = self.slots[self.ring_i]
            if key not in self.pinned:
                break
        if pin:
            self.pinned.add(key)
        self.S.dma("pool", slot[:, 0:nk, :ncols], src.rearrange("(kc p) n -> p kc n", p=128), wr=[key])
        return slot, key

    def transposes(self, src_bf, nblk, rd, dst_fn, dst_keys, eng="act"):
        b = self.newbank()
        pb = self.PSB[b]
        for j in range(nblk):
            self.op("pe", lambda e, j=j: e.transpose(pb[:, j * 128:(j + 1) * 128], src_bf[:, j * 128:(j + 1) * 128], self.ident[:]),
                    rd=list(rd) + ["ident"], wr=[("ps", b)], inc=(j == nblk - 1))
        dst_fn(pb[:, 0:nblk * 128].rearrange("p (c t) -> p c t", c=nblk), ("ps", b))

    def mmg(self, out, pairs, rd, wr, start=True, stop=True):
        n = len(pairs)
        for j, (l, r) in enumerate(pairs):
            self.op("pe", lambda e, l=l, r=r, j=j: e.matmul(out, lhsT=l, rhs=r, start=(start and j == 0),
                                                         stop=(stop and j == n - 1)),
                    rd=rd, wr=wr, inc=(j == n - 1))

    def setup(self):
        nc, S = self.nc, self.S
        din, dout, sb = self.din, self.dout, self.sb
        xin = din("xin", [T, D])
        din("cvec", [128, 16]); din("ident", [128, 128]); din("idlo", [128, 128]); din("idhi", [128, 128])
        din("neg", [128, 64]); din("sel", [2, 256]); din("ropec", [128, 512]); din("ropes", [128, 512])
        din("ada_w", [4, D, 6 * D]); din("adab_col", [4, 128, 96]); din("ada_b", [4, 6 * D])
        din("ln_g", [4, 2, D]); din("ln_b", [4, 2, D]); din("lng_col", [128, 64]); din("lnb_col", [128, 64])
        din("mlp_w1", [4, D, DFF]); din("mlp_w2", [4, DFF, D])
        din("da_w_qkv", [D, 3 * D]); din("da_w_o", [D, D]); din("da_lambda", [1, 256]); din("da_subln_g", [1, 128])
        din("ck_da", [256, D]); din("cv_da", [256, D])
        din("na_w_qkv", [D, 3 * D]); din("na_w_o", [D, D]); din("na_tab", [128, 16 * 2048])
        din("ck_na", [256, D]); din("cv_na", [256, D])
        din("gq_w_qkv", [D, 1536]); din("gq_w_o", [D, D]); din("gq_q_norm", [1, 64]); din("gq_k_norm", [1, 64])
        din("ck_gq", [256, 256]); din("cv_gq", [256, 256])
        din("hy_w_in", [D, 3 * D]); din("hy_w_o", [D, D]); din("hy_sw_col", [128, 72]); din("hy_sb_col", [128, 24])
        din("hy_fb_col", [128, 16]); din("hy_w1", [33, 64]); din("hy_w2", [64, 64]); din("hy_w3", [64, 4096])
        din("hy_cols", [64, 3]); din("hy_log_decay", [1, 4096])
        for Ln in (256, 1024):
            din(f"embT{Ln}", [33, Ln]); din(f"tcol{Ln}", [128, Ln // 128])
            for m in ("Cm", "Sm", "CmT", "SmT"):
                din(f"{m}{Ln}", [Ln, Ln])
            self.dram[f"H{Ln}"] = self.nc.dram_tensor(f"Hscr{Ln}", [2, 2, Ln, D], BF16, kind="Internal").ap()
        dout("y", [T, D])
        for n in ("da", "na"):
            dout(f"st_{n}_k", [512, D]); dout(f"st_{n}_v", [512, D])
        dout("st_gq_k", [512, 256]); dout("st_gq_v", [512, 256])
        d = self.dram
        self.PSall = self.es.enter_context(nc.psum_tensor("psall", [128, 8, 512], F32))
        self.PS = [self.PSall[:, i, :] for i in range(8)]
        self.PSB = [self.PSall[:, i, :].bitcast(BF16) for i in range(8)]
        self.pair_i = 0
        self.x = sb("x", [128, NT, D], F32)
        self.hT = sb("hT", [128, 8, T], BF16)
        self.ring = sb("ring", [128, NSLOT, 8, 512], BF16)
        self.slots = [(self.ring[:, s_], ("ring", s_)) for s_ in range(NSLOT)]
        self.pinned = set()
        self.gate_bc = sb("gate_bc", [128, 2, D], F32)
        self.lnbc = sb("lnbc", [128, 2, D], F32)
        self.ident = sb("ident_sb", [128, 128], BF16)
        self.sel = sb("sel_sb", [2, 256], F32)
        self.cvec = sb("cvec_sb", [128, 16], F32)
        self.siluT = sb("siluT", [128, 16], BF16)
        self.modT = sb("modT", [128, 96], F32)
        self.adabc = sb("adabc", [128, 96], F32)
        self.lngc = sb("lngc", [128, 64], F32)
        self.lnbcol = sb("lnbcol", [128, 64], F32)
        self.colv = sb("colv", [128, 2, 2, 16], F32)
        self.onep = sb("onep", [128, 16], F32)
        self.grow = sb("grow", [2, 2, D], F32)
        self.xnb = sb("xnb", [128, 4, D], BF16)
        self.lnq = []
        self.lazyq = []
        self.lnst = sb("lnst", [128, 2, 2, 6], F32)
        self.lnmv = sb("lnmv", [128, 2, 4], F32)
        self.ln_i = 0
        self.held = set()
        S.dma("pool", self.ident[:], d["ident"], wr=["ident"])
        S.dma("sp", self.sel[:], d["sel"], wr=["sel"])
        S.dma("sp", self.cvec[:], d["cvec"], wr=["cvec"])
        S.dma("sp", self.lngc[:], d["lng_col"], wr=["lngc"])
        S.dma("sp", self.lnbcol[:], d["lnb_col"], wr=["lnbcol"])
        for t in range(NT):
            S.dma("sp", self.x[:, t, :], xin[t * 128:(t + 1) * 128, :], wr=[("x", t)])
        self.op("act", lambda e: e.activation(self.siluT[:], self.cvec[:], AF.Silu), rd=["cvec"], wr=["siluT"])

    def newbank(self):
        while True:
            b = self.bank_i % 7
            self.bank_i += 1
            if b not in self.held:
                return b

    def newpair(self):
        while True:
            b = (self.pair_i % 3) * 2
            self.pair_i += 1
            if b not in self.held and (b + 1) not in self.held:
                return b, self.PSall[:, b:b + 2, :].rearrange("p a n -> p (a n)")

    def ada_gen(self, i):
        d = self.dram
        S = self.S
        siluT = self.siluT[:].rearrange("p (k g) -> p k g", g=2)
        S.dma("sp", self.adabc[:], d["adab_col"][i], wr=["adabc"])
        bT = 7
        for j in range(12):
            slot, rk = self.wload(d["ada_w"][i][:, 512 * j:512 * (j + 1)], 512)
            piece = j // 2
            if piece in (2, 5):
                gi = 0 if piece == 2 else 1
                hf = j % 2
                c0 = 2 * D if gi == 0 else 5 * D
                S.dma("sp", self.grow[:, gi, hf * 512:(hf + 1) * 512],
                      d["ada_b"][i:i + 1, c0 + hf * 512:c0 + (hf + 1) * 512].partition_broadcast(2), wr=[("grow", gi, hf)])
                b = self.newbank()
                self.mmg(self.PS[b][0:2, :], [(siluT[:, kc, :], slot[:, kc, :]) for kc in range(8)],
                         rd=["siluT", rk], wr=[("ps", b)])
                dst = self.grow[:, gi, hf * 512:(hf + 1) * 512]
                self.op("dve", lambda e, b=b, dst=dst, gi=gi, hf=hf: e.tensor_tensor(
                    dst, self.PS[b][0:2, :], dst, ALU.add),
                    rd=[("ps", b), ("grow", gi, hf)], wr=[("grow", gi, hf)])
                self.op("dve", lambda e, dst=dst: e.tensor_scalar_mul(dst, dst, 1.0 / ALPHA),
                        rd=[("grow", gi, hf)], wr=[("grow", gi, hf)])
            else:
                for mb in range(4):
                    fc = 4 * j + mb
                    self.mmg(self.PS[bT][:, 2 * fc:2 * fc + 2],
                             [(slot[:, kc, mb * 128:(mb + 1) * 128], siluT[:, kc, :]) for kc in range(8)],
                             rd=["siluT", rk], wr=[("ps", bT)])
                if j in (3, 9):
                    c0 = 0 if j == 3 else 48
                    self.op("dve", lambda e, c0=c0: e.tensor_tensor(
                        self.modT[:, c0:c0 + 32], self.PS[bT][:, c0:c0 + 32], self.adabc[:, c0:c0 + 32], ALU.add),
                        rd=[("ps", bT), "adabc"], wr=[("modT", c0)])
                    self.colvecs(i, 0 if j == 3 else 1)
            yield j

    def ada0_tick(self):
        if getattr(self, "ada0_it", None) is not None:
            if next(self.ada0_it, None) is None:
                self.ada0_it = None

    def ada_tick(self, limit, n=1):
        for _ in range(n):
            if self.ada_it is not None and self.ada_steps < limit:
                self.ada_steps += 1
                if next(self.ada_it, None) is None:
                    self.ada_it = None

    def colvecs(self, i, slot):
        c0 = 0 if slot == 0 else 48
        mod = self.modT[:, c0:c0 + 32].rearrange("p (a c g) -> p a g c", a=2, g=2)
        onep = self.onep[:].rearrange("p (g c) -> p g c", g=2)
        G = self.colv[:, slot, 0, :].rearrange("p (g c) -> p g c", g=2)
        B = self.colv[:, slot, 1, :].rearrange("p (g c) -> p g c", g=2)
        kk = ("colv", slot)
        self.op("dve", lambda e: e.tensor_scalar_add(onep, mod[:, 1], 1.0), rd=[("modT", c0)], wr=["onep"])
        if slot == 0 and i == 0:
            self.op("dve", lambda e: e.tensor_copy(G, onep), rd=["onep"], wr=[kk])
            self.op("dve", lambda e: e.tensor_copy(B, mod[:, 0]), rd=[("modT", c0), kk], wr=[kk])
            return
        li, lj = (i - 1, 1) if slot == 0 else (i, 0)
        o = (li * 2 + lj) * 8
        gcol = self.lngc[:, o:o + 8].unsqueeze(1).broadcast_to([128, 2, 8])
        bcol = self.lnbcol[:, o:o + 8].unsqueeze(1).broadcast_to([128, 2, 8])
        self.op("dve", lambda e: e.tensor_tensor(G, onep, gcol, ALU.mult), rd=["onep", "lngc"], wr=[kk])
        self.op("dve", lambda e: e.tensor_tensor(B, onep, bcol, ALU.mult), rd=["onep", "lnbcol", kk], wr=[kk])
        self.op("dve", lambda e: e.tensor_tensor(B, B, mod[:, 0], ALU.add), rd=[kk, ("modT", c0)], wr=[kk])

    def gate_bcast(self, gi):
        for g in range(2):
            for hf in range(2):
                b = self.newbank()
                self.mmg(self.PS[b][:], [(self.sel[:, g * 128:(g + 1) * 128], self.grow[:, gi, hf * 512:(hf + 1) * 512])],
                         rd=["sel", ("grow", gi, hf)], wr=[("ps", b)])
                self.op("act", lambda e, b=b, g=g, hf=hf: e.activation(
                    self.gate_bc[:, g, hf * 512:(hf + 1) * 512], self.PS[b][:], AF.Copy),
                    rd=[("ps", b)], wr=[("gate", g)])

    def load_lnbc(self, i, j):
        d = self.dram
        self.S.dma("sp", self.lnbc[:, 0, :], d["ln_g"][i][j:j + 1, :].partition_broadcast(128), wr=["lnbc"])
        self.S.dma("sp", self.lnbc[:, 1, :], d["ln_b"][i][j:j + 1, :].partition_broadcast(128), wr=["lnbc"])

    def to_hT(self, xnb, kx, t, slot):
        g = 0 if t < 4 else 1
        G = self.colv[:, slot, 0, g * 8:(g + 1) * 8]
        B = self.colv[:, slot, 1, g * 8:(g + 1) * 8]

        def dst(pv, pk):
            for c in range(8):
                self.op("act", lambda e, c=c: e.activation(self.hT[:, c, t * 128:(t + 1) * 128], pv[:, c, :], AF.Identity,
                                                           bias=B[:, c:c + 1], scale=G[:, c:c + 1]),
                        rd=[pk, ("colv", slot)], wr=[("hT", t)])
        self.transposes(xnb, 8, [kx], dst, None)

    def first_h(self):
        for t in range(NT):
            p = self.ln_i % 4
            self.ln_i += 1
            xnb, kx = self.xnb[:, p, :], ("xnb", p)
            self.op("act", lambda e, xnb=xnb, t=t: e.activation(xnb, self.x[:, t, :], AF.Copy), rd=[("x", t)], wr=[kx])
            self.ln_push(lambda xnb=xnb, kx=kx, t=t: self.to_hT(xnb, kx, t, 0))
        self.ln_flush()

    def accum_gen(self, t, o_aps, o_keys):
        g = 0 if t < 4 else 1
        kxt = ("x", t)
        for h in range(2):
            self.op("dve", lambda e, h=h: e.tensor_tensor(o_aps[h], o_aps[h], self.gate_bc[:, g, h * 512:(h + 1) * 512], ALU.mult),
                    rd=[o_keys[h], ("gate", g)], wr=[o_keys[h]])
            yield
        for h in range(2):
            xh = self.x[:, t, h * 512:(h + 1) * 512]
            self.op("dve", lambda e, h=h, xh=xh: e.tensor_tensor(xh, o_aps[h], xh, ALU.add), rd=[o_keys[h], kxt], wr=[kxt])
            yield

    def accum_x(self, t, o_aps, o_keys):
        for _ in self.accum_gen(t, o_aps, o_keys):
            pass

    def ln_gen(self, t, o_aps, o_keys, slot, last=False):
        p = self.ln_i % 2
        p3 = self.ln_i % 4
        self.ln_i += 1
        xt, kxt = self.x[:, t, :], ("x", t)
        xnb, kx = self.xnb[:, p3, :], ("xnb", p3)
        st, mv = self.lnst[:, p], self.lnmv[:, p]
        ks = ("lnst", p)
        if o_aps is not None:
            yield from self.accum_gen(t, o_aps, o_keys)
        for h in range(2):
            self.op("dve", lambda e, h=h: e.bn_stats(st[:, h, :], xt[:, h * 512:(h + 1) * 512]), rd=[kxt], wr=[ks])
            yield
        self.op("dve", lambda e: e.bn_aggr(mv[:, 0:2], st), rd=[ks], wr=[ks])
        yield
        self.op("dve", lambda e: e.tensor_scalar_add(mv[:, 2:3], mv[:, 1:2], LN_EPS), rd=[ks], wr=[ks])
        yield
        self.op("act", lambda e: e.sqrt(mv[:, 2:3], mv[:, 2:3]), rd=[ks], wr=[ks])
        yield
        self.op("dve", lambda e: e.reciprocal(mv[:, 2:3], mv[:, 2:3]), rd=[ks], wr=[ks])
        yield
        self.op("dve", lambda e: e.tensor_scalar(mv[:, 3:4], mv[:, 0:1], mv[:, 2:3], -1.0, ALU.mult, ALU.mult),
                rd=[ks], wr=[ks])
        yield
        if not last:
            self.op("act", lambda e: e.activation(xnb, xt, AF.Identity, bias=mv[:, 3:4], scale=mv[:, 2:3]),
                    rd=[kxt, ks], wr=[kx])
            yield
        self.op("act", lambda e: e.activation(xt, xt, AF.Identity, bias=mv[:, 3:4], scale=mv[:, 2:3]),
                rd=[kxt, ks], wr=[kxt])
        yield
        def affine():
            self.op("dve", lambda e: e.tensor_tensor(xt, xt, self.lnbc[:, 0, :], ALU.mult), rd=[kxt, "lnbc"], wr=[kxt])
            self.op("dve", lambda e: e.tensor_tensor(xt, xt, self.lnbc[:, 1, :], ALU.add), rd=[kxt, "lnbc"], wr=[kxt])
        if last:
            affine()
            yield
            self.S.dma("sp", self.dram["y"][t * 128:(t + 1) * 128, :], xt, rd=[kxt])
        else:
            self.lazyq.append(affine)
            self.ln_push(lambda: self.to_hT(xnb, kx, t, slot))

    def ln_tiles(self, items):
        gens = [self.ln_gen(*it) for it in items]
        while gens:
            for g_ in list(gens):
                try:
                    next(g_)
                except StopIteration:
                    gens.remove(g_)

    def ln_tile(self, t, o_aps, o_keys, slot, last=False):
        self.ln_tiles([(t, o_aps, o_keys, slot, last)])

    def lazy_flush(self, n=None):
        while self.lazyq and (n is None or n > 0):
            self.lazyq.pop(0)()
            if n is not None:
                n -= 1

    def ln_push(self, fn):
        self.lnq.append(fn)
        if len(self.lnq) > 2:
            self.lnq.pop(0)()

    def ln_flush(self):
        while self.lnq:
            self.lnq.pop(0)()

    def mlp_phase(self, i, ada_it, last):
        d = self.dram
        ph = ExitStack()
        uT = self.sb("uT", [128, 8, T], BF16, ph)
        r32 = self.sb("r32", [128, 2, 512], F32, ph)
        ring2 = self.sb("ring2", [128, 4, 8, 512], BF16, ph)
        self.slots = self.slots[:NSLOT] + [(ring2[:, s_], ("ring2", s_)) for s_ in range(4)]
        self.ring_i = len(self.slots) - 1
        self.gate_bcast(1)
        w1, w2 = d["mlp_w1"][i], d["mlp_w2"][i]
        ri = 0

        def ada_step():
            self.ada_tick(12)
        for fb in range(4):
            for cc in range(2):
                c = 2 * fb + cc
                slot, rk = self.wload(w1[:, 512 * c:512 * (c + 1)], 512)
                for tb in range(3):
                    if tb == 2:
                        self.ln_flush()
                    hk = [("hT", tb * 4 + q) for q in range(4)]
                    for mb in range(4):
                        self.lazy_flush(1)
                        b = self.newbank()
                        self.mmg(self.PS[b][:],
                                 [(slot[:, kc, mb * 128:(mb + 1) * 128], self.hT[:, kc, tb * 512:(tb + 1) * 512]) for kc in range(8)],
                                 rd=hk + [rk], wr=[("ps", b)])
                        rr, kr = r32[:, ri % 2, :], ("r32", ri % 2)
                        ri += 1
                        self.op("act", lambda e, b=b, rr=rr: e.activation(rr, self.PS[b][:], AF.Relu),
                                rd=[("ps", b)], wr=[kr])
                        ud = uT[:, cc * 4 + mb, tb * 512:(tb + 1) * 512]
                        self.op("dve", lambda e, rr=rr, ud=ud: e.tensor_tensor(ud, rr, rr, ALU.mult),
                                rd=[kr], wr=[("uT", cc * 4 + mb, tb)])
                ada_step()
            if fb == 0:
                self.lazy_flush()
                self.load_lnbc(i, 1)
            sl2 = [self.wload(w2[fb * 1024:(fb + 1) * 1024, hf * 512:(hf + 1) * 512], 512) for hf in range(2)]
            items = []
            for t in range(NT):
                bs = []
                uk = [("uT", kc, t // 4) for kc in range(8)]
                for hf in range(2):
                    slot, rk = sl2[hf]
                    b = self.newbank()
                    self.held.add(b)
                    self.mmg(self.PS[b][:], [(uT[:, kc, t * 128:(t + 1) * 128], slot[:, kc, :]) for kc in range(8)],
                             rd=uk + [rk], wr=[("ps", b)])
                    bs.append(b)
                items.append((t, [self.PS[b][:] for b in bs], [("ps", b) for b in bs], 0, last))
                if len(items) == 2:
                    if fb == 3:
                        self.ln_tiles(items)
                    else:
                        gens = [self.accum_gen(it[0], it[1], it[2]) for it in items]
                        while gens:
                            for g_ in list(gens):
                                try:
                                    next(g_)
                                except StopIteration:
                                    gens.remove(g_)
                    for it in items:
                        for kb_ in it[2]:
                            self.held.discard(kb_[1])
                    items = []
            ada_step()
        self.S.retire(lambda k: isinstance(k, tuple) and k[0] in ("uT", "r32", "ring2"))
        self.slots = self.slots[:NSLOT]
        self.ring_i = self.ring_i % NSLOT
        ph.close()

    ATT = {
        "da": dict(w="da_w_qkv", wo="da_w_o", ck="ck_da", cv="cv_da", nh=8, dv=128, nvh=8,
                   chunks=[("q", 0, 512), ("q", 512, 512), ("k", 1024, 512), ("k", 1536, 512),
                           ("v", 2048, 512), ("v", 2560, 512)]),
        "na": dict(w="na_w_qkv", wo="na_w_o", ck="ck_na", cv="cv_na", nh=16, dv=64, nvh=16,
                   chunks=[("q", 0, 512), ("q", 512, 512), ("k", 1024, 512), ("k", 1536, 512),
                           ("v", 2048, 512), ("v", 2560, 512)]),
        "gq": dict(w="gq_w_qkv", wo="gq_w_o", ck="ck_gq", cv="cv_gq", nh=16, dv=64, nvh=4,
                   chunks=[("q", 0, 512), ("q", 512, 512), ("kv", 1024, 512)]),
    }

    def rope(self, src, skeys, lt, dst, dkey):
        A, B = self.rtA[:], self.rtB[:]
        cosb = self.ropec[:, lt, :].unsqueeze(1).broadcast_to([128, 8, 64])
        v8 = lambda ap: ap.rearrange("p (g f) -> p g f", g=8)
        v5 = lambda ap: ap.rearrange("p (g a h j) -> p g a h j", g=8, a=2, h=2, j=16)
        sinv = self.ropes[:, lt, :].rearrange("p (a h j) -> p a h j", a=2, h=2, j=16)
        hA = [("rtAh", 0), ("rtAh", 1)]
        hB = [("rtBh", 0), ("rtBh", 1)]
        self.op("dve", lambda e: e.tensor_tensor(v8(A), v8(src), cosb, ALU.mult), rd=list(skeys) + ["ropec"], wr=["rtA"] + hA)
        for h in range(2):
            sb_ = sinv[:, :, h, :].unsqueeze(1).broadcast_to([128, 8, 2, 16])
            self.op("dve", lambda e, h=h, sb_=sb_: e.tensor_tensor(v5(B)[:, :, :, h, :], v5(src)[:, :, :, 1 - h, :], sb_, ALU.mult),
                    rd=list(skeys) + ["ropes"], wr=["rtB"] + hB)
        self.op("dve", lambda e: e.tensor_tensor(dst, A, B, ALU.add), rd=["rtA", "rtB"], wr=[dkey])

    def rmsn(self, src, skey, nh, normbc, nkey, dst, dkey):
        sq = self.rtA[:, 0:nh * 64]
        ss = self.rstat[:, 0:nh]
        v = lambda ap: ap.rearrange("p (g f) -> p g f", g=nh)
        self.op("act", lambda e: e.activation(sq, src, AF.Square), rd=[skey], wr=["rtA"])
        self.op("dve", lambda e: e.tensor_reduce(ss, v(sq), AX.X, ALU.add), rd=["rtA"], wr=["rstat"])
        self.op("dve", lambda e: e.tensor_scalar(ss, ss, 1.0 / 64, RMS_EPS, ALU.mult, ALU.add), rd=["rstat"], wr=["rstat"])
        self.op("act", lambda e: e.sqrt(ss, ss), rd=["rstat"], wr=["rstat"])
        self.op("dve", lambda e: e.reciprocal(ss, ss), rd=["rstat"], wr=["rstat"])
        self.op("dve", lambda e: e.tensor_tensor(v(dst), v(src), ss.unsqueeze(2).broadcast_to([128, nh, 64]), ALU.mult),
                rd=[skey, "rstat"], wr=[dkey])
        self.op("dve", lambda e: e.tensor_tensor(v(dst), v(dst), normbc.unsqueeze(1).broadcast_to([128, nh, 64]), ALU.mult),
                rd=[dkey, nkey], wr=[dkey])

    def state_out(self, name, src_ps, skey, t, col0, ncols):
        p = 0
        sg, kg = self.stg[:, p, 0:ncols], ("stg", p)
        self.op("act", lambda e: e.activation(sg, src_ps, AF.Copy), rd=[skey], wr=[kg])
        self.S.dma("sp", self.dram[name][t * 128:(t + 1) * 128, col0:col0 + ncols], sg, rd=[kg])

    def attn_layer(self, kind, i):
        L = self.ATT[kind]
        d = self.dram
        nh, dv, nvh = L["nh"], L["dv"], L["nvh"]
        ph = ExitStack()
        sb = lambda n, s, dt: self.sb(n, s, dt, ph)
        self.qT = sb("qT", [128, 8, 512], BF16)
        self.kT = sb("kT", [128, 8, 1280], BF16)
        self.V = sb("Vaug", [128, 10, nvh, dv + 1], BF16)
        self.Otok = sb("Otok", [128, 1, 4 if kind == "na" else 2, D], BF16)
        self.PT = sb("PT", [128, 2, 4 if kind == "na" else 10, 256], BF16)
        if kind != "na":
            self.rtA = sb("rtA", [128, 512], F32)
            self.rtB = sb("rtB", [128, 512], F32)
            self.ropec = sb("ropec_sb", [128, 8, 64], F32)
            self.ropes = sb("ropes_sb", [128, 8, 64], F32)
            self.S.dma("sp", self.ropec[:], d["ropec"].rearrange("p (a b) -> p a b", a=8), wr=["ropec"])
            self.S.dma("sp", self.ropes[:], d["ropes"].rearrange("p (a b) -> p a b", a=8), wr=["ropes"])
        self.rstat = sb("rstat", [128, 16], F32)
        self.qkst = sb("qkst", [128, 3, 512], BF16)
        self.stg = sb("stg", [128, 1, 512], F32)
        self.asm = sb("asm", [128, 2, 8], F32)
        self.stg_i = 0
        self.qk_i = 0
        self.qkq = []
        self.att_i = 0
        if kind == "da":
            self.lamt = self.rtA[:, 0:256]
            self.lamv = sb("lamv", [128, 4], F32)
            self.gsub = sb("gsub", [128, 128], F32)
            self.S.dma("sp", self.lamt, d["da_lambda"].partition_broadcast(128), wr=["rtA"])
            self.S.dma("sp", self.gsub[:], d["da_subln_g"].partition_broadcast(128), wr=["gsub"])
            lam_init = 0.8 - 0.6 * float(np.exp(-0.3 * i))
            self.lam_init = lam_init
            lt_ = self.lamt
            for j in range(2):
                self.op("dve", lambda e, j=j: e.tensor_tensor(lt_[:, j * 128:j * 128 + 64], lt_[:, j * 128:j * 128 + 64],
                                                             lt_[:, j * 128 + 64:j * 128 + 128], ALU.mult), rd=["rtA"], wr=["rtA"])
                self.op("dve", lambda e, j=j: e.tensor_reduce(self.lamv[:, j:j + 1], lt_[:, j * 128:j * 128 + 64], AX.X, ALU.add),
                        rd=["rtA"], wr=["lamv"])
            self.op("act", lambda e: e.activation(self.lamv[:, 0:2], self.lamv[:, 0:2], AF.Exp), rd=["lamv"], wr=["lamv"])
            self.op("dve", lambda e: e.tensor_tensor(self.lamv[:, 2:3], self.lamv[:, 1:2], self.lamv[:, 0:1], ALU.subtract),
                    rd=["lamv"], wr=["lamv"])
            self.op("dve", lambda e: e.tensor_scalar_add(self.lamv[:, 2:3], self.lamv[:, 2:3], -lam_init), rd=["lamv"], wr=["lamv"])
            self.op("dve", lambda e: e.tensor_scalar_mul(self.gsub[:], self.gsub[:], 1.0 - lam_init), rd=["gsub"], wr=["gsub"])
        if kind == "gq":
            ring3 = sb("ring3", [128, 1, 8, 512], BF16)
            self.slots = self.slots[:NSLOT] + [(ring3[:, s_], ("ring3", s_)) for s_ in range(1)]
            self.qn = sb("qn", [128, 64], F32)
            self.kn = sb("kn", [128, 64], F32)
            self.nrm = sb("nrm", [128, 512], F32)
            self.kdup = sb("kdup", [128, 3, 4, 2, 64], BF16)
            self.S.dma("sp", self.qn[:], d["gq_q_norm"].partition_broadcast(128), wr=["qn"])
            self.S.dma("sp", self.kn[:], d["gq_k_norm"].partition_broadcast(128), wr=["kn"])
        if kind == "na":
            self.tab = sb("tab", [128, 2, 16, 2, 64], BF16)
            self.idlo = sb("idlo_sb", [128, 128], BF16)
            self.idhi = sb("idhi_sb", [128, 128], BF16)
            self.neg = sb("neg_sb", [128, 64], BF16)
            self.S.dma("pool", self.idlo[:], d["idlo"], wr=["idlo"])
            self.S.dma("pool", self.idhi[:], d["idhi"], wr=["idhi"])
            self.S.dma("pool", self.neg[:], d["neg"], wr=["neg"])
        self.op("dve", lambda e: e.memset(self.V[:, :, :, dv:dv + 1], 1.0), rd=[], wr=[("V", j) for j in range(10)])
        self.gate_bcast(0)
        wo = d[L["wo"]]
        for pi in range(3):
            sample = pi > 0
            if pi == 0:
                tiles = [0, 1, 2, 3]
                self.qkv_pass(kind, L, tiles, False, ("q", "k", "v", "kv"), 0)
                self.ln_flush()
                self.lazy_flush()
                self.load_lnbc(i, 0)
                slots = [self.wload(wo[:, hf * 512:(hf + 1) * 512], 512, pin=True) for hf in range(2)]
                for lt0 in (0, 2):
                    self.dense_attn(kind, L, lt0, 2, [lt0, lt0 + 1])
            else:
                qh = pi - 1
                if qh == 0:
                    self.qkv_pass(kind, L, list(range(4, 12)), True, ("k", "v", "kv"), 0)
                    self.load_cache(kind, L)
                tiles = list(range(4 + 4 * qh, 8 + 4 * qh))
                self.qkv_pass(kind, L, tiles, True, ("q",), 4 * qh)
                slots = [self.wload(wo[:, hf * 512:(hf + 1) * 512], 512, pin=True) for hf in range(2)]
                if kind == "na":
                    self.na_sample_attn(L, 4 * qh)
                else:
                    self.dense_attn(kind, L, 0, 4, list(range(10)))
            items = []
            for lt, t in enumerate(tiles):
                bs = []
                for hf in range(2):
                    b = self.newbank()
                    self.held.add(b)
                    slot, rk = slots[hf]
                    self.mmg(self.PS[b][:], [(self.qT[:, kc, lt * 128:(lt + 1) * 128], slot[:, kc, :]) for kc in range(8)],
                             rd=[("qT", lt), rk], wr=[("ps", b)])
                    bs.append(b)
                items.append((t, [self.PS[b][:] for b in bs], [("ps", b) for b in bs], 1))
                if len(items) == 2 or lt == len(tiles) - 1:
                    self.ln_tiles(items)
                    for it in items:
                        for kb_ in it[2]:
                            self.held.discard(kb_[1])
                    items = []
            for (_s, k_) in slots:
                self.pinned.discard(k_)
            self.ada_tick(9)
        names = ("qT", "kT", "V", "Otok", "PT", "rtA", "rtB", "rstat", "qkst", "stg", "cst", "asm", "of32", "lamt", "lamv",
                 "gsub", "qn", "kn", "nrm", "kdup", "tab", "idlo", "idhi", "neg", "sqd", "rtA2", "ropec", "ropes", "rtAh", "rtBh", "ring3")
        self.S.retire(lambda k: (k in names) or (isinstance(k, tuple) and k[0] in names))
        self.slots = self.slots[:NSLOT]
        self.ring_i = self.ring_i % NSLOT
        ph.close()

    def qk_to_T(self, src_bf, skey, dstT, c0, lt, dkey, defer=True):
        def run():
            def dst(pv, pk):
                self.op("act", lambda e: e.activation(dstT[:, c0:c0 + 4, lt * 128:(lt + 1) * 128], pv, AF.Copy),
                        rd=[pk], wr=[(dkey, lt)])
            self.transposes(src_bf, 4, [skey], dst, None)
        self.qkq.append(run)
        if len(self.qkq) > (1 if defer else 0):
            self.qkq.pop(0)()

    def qk_flush(self):
        while self.qkq:
            self.qkq.pop(0)()

    def qkv_pass(self, kind, L, tiles, sample, which, rope0):
        self._qkv_pass(kind, L, tiles, sample, which, rope0)
        self.qk_flush()

    def _qkv_pass(self, kind, L, tiles, sample, which, rope0):
        d = self.dram
        w = d[L["w"]]
        dv, nvh = L["dv"], L["nvh"]
        for (cn, col0, ncols) in L["chunks"]:
            if cn not in which:
                continue
            slot, rk = self.wload(w[:, col0:col0 + ncols], ncols)
            for lt, t in enumerate(tiles):
                self.lazy_flush(1)
                b = self.newbank()
                ps, pk = self.PS[b][:], ("ps", b)
                self.mmg(ps, [(self.hT[:, kc, t * 128:(t + 1) * 128], slot[:, kc, :]) for kc in range(8)],
                         rd=[("hT", t), rk], wr=[pk])
                qi = self.qk_i % 3
                self.qk_i += 1
                st, ks = self.qkst[:, qi, :], ("qkst", qi)
                if cn in ("q", "k"):
                    fc0 = (col0 % 1024) // 128
                    if kind == "gq":
                        self.rmsn(ps, pk, 8, self.qn[:], "qn", self.nrm[:], "nrm")
                        if sample:
                            self.rope(self.nrm[:], ["nrm"], rope0 + lt, st, ks)
                        else:
                            self.op("act", lambda e: e.activation(st, self.nrm[:], AF.Copy), rd=["nrm"], wr=[ks])
                    else:
                        if cn == "k" and not sample:
                            self.state_out(f"st_{kind}_k", ps, pk, t, col0 - 1024, 512)
                        if kind == "da" and sample:
                            self.rope(ps, [pk], rope0 + lt, st, ks)
                        else:
                            self.op("act", lambda e: e.activation(st, ps, AF.Copy), rd=[pk], wr=[ks])
                    self.qk_to_T(st, ks, self.qT if cn == "q" else self.kT, fc0, lt, "qT" if cn == "q" else "kT")
                elif cn == "v":
                    h0 = (col0 - 2048) // dv
                    nhc = 512 // dv
                    if not sample:
                        self.state_out(f"st_{kind}_v", ps, pk, t, col0 - 2048, 512)
                    self.op("act", lambda e, h0=h0, nhc=nhc: e.activation(
                        self.V[:, lt, h0:h0 + nhc, 0:dv], ps.rearrange("p (h f) -> p h f", h=nhc), AF.Copy),
                        rd=[pk], wr=[("V", lt)])
                else:
                    kn_, kk = self.nrm[:, 0:256], "nrm"
                    self.rmsn(ps[:, 0:256], pk, 4, self.kn[:], "kn", kn_, kk)
                    if not sample:
                        p = 0
                        sg, kg = self.stg[:, p, 0:256], ("stg", p)
                        self.op("act", lambda e: e.activation(sg, kn_, AF.Copy), rd=[kk], wr=[kg])
                        self.S.dma("sp", d["st_gq_k"][t * 128:(t + 1) * 128, :], sg, rd=[kg])
                        self.state_out("st_gq_v", ps[:, 256:512], pk, t, 0, 256)
                        ksrc, kkeys = kn_, [kk]
                    else:
                        self.op("act", lambda e: e.activation(self.nrm[:, 256:512], self.nrm[:, 0:256], AF.Copy), rd=[kk], wr=[kk])
                        self.rope(self.nrm[:], [kk], rope0 + lt, self.rtA[:], "rtA2")
                        ksrc, kkeys = self.rtA[:, 0:256], ["rtA2", "rtA"]
                    kd = self.kdup[:, qi]
                    for dup in range(2):
                        self.op("act", lambda e, dup=dup: e.activation(
                            kd[:, :, dup, :], ksrc.rearrange("p (h f) -> p h f", h=4), AF.Copy),
                            rd=kkeys, wr=[("kdup", qi)])
                    self.qk_to_T(kd.rearrange("p h u f -> p (h u f)"), ("kdup", qi), self.kT, 0, lt, "kT")
                    self.op("act", lambda e: e.activation(
                        self.V[:, lt, 0:4, 0:64], ps[:, 256:512].rearrange("p (h f) -> p h f", h=4), AF.Copy),
                        rd=[pk], wr=[("V", lt)])

    def load_cache(self, kind, L):
        d = self.dram
        dv, nvh = L["dv"], L["nvh"]
        ck, cv = d[L["ck"]], d[L["cv"]]
        for j in range(2):
            lt = 8 + j
            if kind == "gq":
                qi = self.qk_i % 3
                self.qk_i += 1
                kd = self.kdup[:, qi]
                for dup in range(2):
                    self.S.dma("pool", kd[:, :, dup, :], ck[j * 128:(j + 1) * 128, :].rearrange("p (h f) -> p h f", h=4),
                               wr=[("kdup", qi)])
                self.qk_to_T(kd.rearrange("p h u f -> p (h u f)"), ("kdup", qi), self.kT, 0, lt, "kT")
            else:
                for hf in range(2):
                    qi = self.qk_i % 3
                    self.qk_i += 1
                    self.S.dma("pool", self.qkst[:, qi, :], ck[j * 128:(j + 1) * 128, hf * 512:(hf + 1) * 512], wr=[("qkst", qi)])
                    self.qk_to_T(self.qkst[:, qi, :], ("qkst", qi), self.kT, hf * 4, lt, "kT")
            self.S.dma("pool", self.V[:, lt, :, 0:dv], cv[j * 128:(j + 1) * 128, :].rearrange("p (h f) -> p h f", h=nvh),
                       wr=[("V", lt)])
        self.qk_flush()

    def head_ops(self, kind, h, c):
        if kind == "da":
            return slice(c * 64, c * 64 + 64), h, h, h
        if kind == "na":
            return slice((h % 2) * 64, (h % 2) * 64 + 64), h // 2, h // 2, h
        return slice((h % 2) * 64, (h % 2) * 64 + 64), h // 2, h // 4, h // 4

    def dense_attn(self, kind, L, lt0, nt, ktiles):
        nh, dv = L["nh"], L["dv"]
        ncomp = 2 if kind == "da" else 1
        nk = len(ktiles)
        units = [(h, c) for h in range(nh) for c in range(ncomp)]
        for qb in range(nt // 2):
            q0 = (lt0 + qb * 2) * 128
            qtl = [lt0 + qb * 2, lt0 + qb * 2 + 1]
            oi = 0
            Ot = self.Otok[:, oi]

            def stage_s(ui):
                h, c = units[ui]
                psl, qc, kc_, vh = self.head_ops(kind, h, c)
                pset = ui % 2
                for k0 in range(0, nk, 2):
                    kk = ktiles[k0:k0 + 2]
                    b = self.newbank()
                    for jj, kt in enumerate(kk):
                        self.mmg(self.PS[b][:, jj * 256:(jj + 1) * 256],
                                 [(self.kT[psl, kc_, kt * 128:(kt + 1) * 128], self.qT[psl, qc, q0:q0 + 256])],
                                 rd=[("kT", kt), ("qT", qtl[0]), ("qT", qtl[1])], wr=[("ps", b)])
                    n = len(kk)
                    self.op("act", lambda e, b=b, n=n, k0=k0: e.activation(
                        self.PT[:, pset, k0:k0 + n, :], self.PS[b][:, 0:n * 256].rearrange("p (a q) -> p a q", a=n),
                        AF.Exp, scale=0.125), rd=[("ps", b)], wr=[("PT", pset)])

            def stage_v(ui, accs):
                h, c = units[ui]
                psl, qc, kc_, vh = self.head_ops(kind, h, c)
                pset = ui % 2
                b = self.newbank()
                self.held.add(b)
                for qt in range(2):
                    self.mmg(self.PS[b][:, qt * (dv + 1):(qt + 1) * (dv + 1)],
                             [(self.PT[:, pset, k_i, qt * 128:(qt + 1) * 128], self.V[:, kt, vh, :]) for k_i, kt in enumerate(ktiles)],
                             rd=[("PT", pset)] + [("V", kt) for kt in ktiles], wr=[("ps", b)])
                accs.append(b)
                if len(accs) == ncomp:
                    g_ = self.attn_evac_gen(kind, h, list(accs), Ot, oi, dv)
                    if kind == "da" and pend[0] is None:
                        pend[0] = (g_, list(accs))
                    else:
                        gl, bl = [g_], list(accs)
                        if pend[0] is not None:
                            gl = [pend[0][0], g_]
                            bl += pend[0][1]
                            pend[0] = None
                        self.run_gens(gl)
                        for bb in bl:
                            self.held.discard(bb)
                    accs.clear()

            accs = []
            pend = [None]
            stage_s(0)
            for ui in range(len(units)):
                if ui + 1 < len(units):
                    stage_s(ui + 1)
                stage_v(ui, accs)
            if pend[0] is not None:
                self.run_gens([pend[0][0]])
                for bb in pend[0][1]:
                    self.held.discard(bb)
                pend[0] = None
            self.ada_tick(9)
            for qt in range(2):
                lt = qtl[qt]

                def dst(pv, pk, lt=lt):
                    self.op("act", lambda e: e.activation(self.qT[:, :, lt * 128:(lt + 1) * 128], pv, AF.Copy),
                            rd=[pk], wr=[("qT", lt)])
                self.transposes(Ot[:, qt, :], 8, [("Otok", oi)], dst, None)

    def attn_evac_gen(self, kind, h, accs, Ot, oi, dv):
        ai = self.att_i2 = getattr(self, "att_i2", 0) + 1
        par = ai % 2
        sm = self.asm[:, par]
        ksm = ("asm", par)
        bc = lambda ap, n: ap.unsqueeze(2).broadcast_to([128, 2, n])
        if kind != "da":
            b = accs[0]
            A = self.PS[b][:, 0:2 * (dv + 1)].rearrange("p (q c) -> p q c", c=dv + 1)
            self.op("dve", lambda e: e.reciprocal(sm[:, 0:2], A[:, :, dv]), rd=[("ps", b)], wr=[ksm])
            self.op("dve", lambda e: e.tensor_tensor(Ot[:, 0:2, h * 64:(h + 1) * 64], A[:, :, 0:dv], bc(sm[:, 0:2], dv), ALU.mult),
                    rd=[("ps", b), ksm], wr=[("Otok", oi)])
            return
        b1, b2 = accs
        A1 = self.PS[b1][:, 0:258].rearrange("p (q c) -> p q c", c=129)
        A2 = self.PS[b2][:, 0:258].rearrange("p (q c) -> p q c", c=129)
        of = self.rtA[:, par * 256:(par + 1) * 256].rearrange("p (q c) -> p q c", c=128)
        t2 = self.rtB[:, par * 256:(par + 1) * 256].rearrange("p (q c) -> p q c", c=128)
        ko, kt2 = ("rtAh", par), ("rtBh", par)
        self.op("dve", lambda e: e.reciprocal(sm[:, 0:2], A1[:, :, 128]), rd=[("ps", b1)], wr=[ksm])
        yield
        self.op("dve", lambda e: e.reciprocal(sm[:, 2:4], A2[:, :, 128]), rd=[("ps", b2)], wr=[ksm])
        yield
        self.op("dve", lambda e: e.tensor_scalar_mul(sm[:, 2:4], sm[:, 2:4], self.lamv[:, 2:3]), rd=[ksm, "lamv"], wr=[ksm])
        yield
        self.op("dve", lambda e: e.tensor_tensor(of, A1[:, :, 0:128], bc(sm[:, 0:2], 128), ALU.mult), rd=[("ps", b1), ksm], wr=[ko])
        yield
        self.op("dve", lambda e: e.tensor_tensor(t2, A2[:, :, 0:128], bc(sm[:, 2:4], 128), ALU.mult), rd=[("ps", b2), ksm], wr=[kt2])
        yield
        self.op("dve", lambda e: e.tensor_tensor(of, of, t2, ALU.add), rd=[ko, kt2], wr=[ko])
        yield
        self.op("dve", lambda e: e.tensor_tensor(t2, of, of, ALU.mult), rd=[ko, kt2], wr=[kt2])
        yield
        self.op("dve", lambda e: e.tensor_reduce(sm[:, 4:6], t2, AX.X, ALU.add), rd=[kt2], wr=[ksm])
        yield
        self.op("dve", lambda e: e.tensor_scalar(sm[:, 4:6], sm[:, 4:6], 1.0 / 128, RMS_EPS, ALU.mult, ALU.add), rd=[ksm], wr=[ksm])
        yield
        self.op("act", lambda e: e.activation(sm[:, 4:6], sm[:, 4:6], AF.Ln), rd=[ksm], wr=[ksm])
        yield
        self.op("act", lambda e: e.activation(sm[:, 4:6], sm[:, 4:6], AF.Exp, scale=-0.5), rd=[ksm], wr=[ksm])
        yield
        self.op("dve", lambda e: e.tensor_tensor(of, of, bc(sm[:, 4:6], 128), ALU.mult), rd=[ko, ksm], wr=[ko])
        yield
        self.op("dve", lambda e: e.tensor_tensor(Ot[:, 0:2, h * 128:(h + 1) * 128], of,
                                                 self.gsub[:].unsqueeze(1).broadcast_to([128, 2, 128]), ALU.mult),
                rd=[ko, "gsub"], wr=[("Otok", oi)])
        yield


    def run_gens(self, gens):
        gens = list(gens)
        while gens:
            for g_ in list(gens):
                try:
                    next(g_)
                except StopIteration:
                    gens.remove(g_)

    def na_sample_attn(self, L, m0):
        PTf = self.PT[:].rearrange("p s a q -> p s (a q)")
        d = self.dram
        units = [(h, m) for h in range(16) for m in range(m0, m0 + 4)]

        def geom(m):
            rows = (2 * m, 2 * m + 1)
            rs = [min(max(r - 4, 0), 8) for r in rows]
            chunks = list(range(min(rs) // 2, (max(rs) + 7) // 2 + 1))
            return rows, rs, chunks + [8, 9]

        def stage_s(ui):
            h, m = units[ui]
            ml = m - m0
            rows, rs, allc = geom(m)
            if ml == 0:
                self.S.dma("pool", self.tab[:, h % 2].rearrange("p b a c -> p (b a c)"), d["na_tab"][:, h * 2048:(h + 1) * 2048],
                           wr=[("tab", h % 2)])
            tabh, tk = self.tab[:, h % 2], ("tab", h % 2)
            psl = slice((h % 2) * 64, (h % 2) * 64 + 64)
            qc = h // 2
            pset = ui % 2
            for bi in range(0, len(allc), 4):
                cc = allc[bi:bi + 4]
                b = self.newbank()
                for jj, c in enumerate(cc):
                    out = self.PS[b][:, jj * 128:(jj + 1) * 128]
                    mms = [(out, self.kT[psl, qc, c * 128:(c + 1) * 128], self.qT[psl, qc, ml * 128:(ml + 1) * 128], [("kT", c), ("qT", ml)])]
                    if c < 8:
                        kb = 2 * c
                        ep = kb - 2 * m + 7
                        mms.append((out, self.ident[:], tabh[:, ep].rearrange("p a c -> p (a c)"), ["ident", tk]))
                        for a in range(2):
                            v0 = rs[a] <= kb < rs[a] + 8
                            v1 = rs[a] <= kb + 1 < rs[a] + 8
                            oa = self.PS[b][:, jj * 128 + a * 64:jj * 128 + (a + 1) * 64]
                            if v0 or v1:
                                if not v0:
                                    mms.append((oa, self.idlo[:], self.neg[:], ["idlo", "neg"]))
                                if not v1:
                                    mms.append((oa, self.idhi[:], self.neg[:], ["idhi", "neg"]))
                            else:
                                mms.append((oa, self.ident[:], self.neg[:], ["ident", "neg"]))
                    n = len(mms)
                    for j, (o_, l_, r_, rd_) in enumerate(mms):
                        self.op("pe", lambda e, o_=o_, l_=l_, r_=r_, j=j, n=n: e.matmul(o_, lhsT=l_, rhs=r_, start=(j == 0), stop=(j == n - 1)),
                                rd=rd_, wr=[("ps", b)], inc=(j == n - 1))
                ncol = len(cc) * 128
                self.op("act", lambda e, b=b, bi=bi, ncol=ncol: e.activation(
                    PTf[:, pset, bi * 128:bi * 128 + ncol], self.PS[b][:, 0:ncol], AF.Exp, scale=0.125),
                    rd=[("ps", b)], wr=[("PT", pset)])

        def stage_v(ui):
            h, m = units[ui]
            ml = m - m0
            rows, rs, allc = geom(m)
            pset = ui % 2
            b = self.newbank()
            self.held.add(b)
            self.mmg(self.PS[b][:, 0:65],
                     [(PTf[:, pset, ci * 128:(ci + 1) * 128], self.V[:, c, h, :]) for ci, c in enumerate(allc)],
                     rd=[("PT", pset)] + [("V", c) for c in allc], wr=[("ps", b)])
            ai = self.att_i2 = getattr(self, "att_i2", 0) + 1
            sm, ksm = self.asm[:, ai % 2], ("asm", ai % 2)
            self.op("dve", lambda e: e.reciprocal(sm[:, 0:1], self.PS[b][:, 64:65]), rd=[("ps", b)], wr=[ksm])
            self.op("act", lambda e: e.activation(self.Otok[:, 0, ml, h * 64:(h + 1) * 64], self.PS[b][:, 0:64], AF.Copy, scale=sm[:, 0:1]),
                    rd=[("ps", b), ksm], wr=[("Otok", ml)])
            self.held.discard(b)

        stage_s(0)
        for ui in range(len(units)):
            if ui + 1 < len(units):
                stage_s(ui + 1)
            stage_v(ui)
            if ui % 16 == 15:
                self.ada_tick(9)
        for ml in range(4):
            def dst(pv, pk, ml=ml):
                self.op("act", lambda e: e.activation(self.qT[:, :, ml * 128:(ml + 1) * 128], pv, AF.Copy),
                        rd=[pk], wr=[("qT", ml)])
            self.transposes(self.Otok[:, 0, ml, :], 8, [("Otok", ml)], dst, None)

    def build(self):
        self.setup()
        self.ada0_it = self.ada_gen(0)
        if self.n_layers >= 4:
            self.hyena_filters()
        while self.ada0_it is not None:
            self.ada0_tick()
        self.first_h()
        kinds = ["da", "na", "gq", "hy"]
        self.ada_it = None
        for i in range(self.n_layers):
            self.cur_layer = i
            self.ada_it = self.ada_gen(i + 1) if i + 1 < self.n_layers else None
            self.ada_steps = 0
            if kinds[i] == "hy":
                self.hyena_layer(i)
            else:
                self.attn_layer(kinds[i], i)
            last = i == self.n_layers - 1
            self.mlp_phase(i, None, last)
            self.ada_tick(12, 12)
        self.S.finish()
        return self.nc


def _host_consts():
    c = {}
    c["ident"] = np.eye(128, dtype=np.float32)
    lo = np.zeros((128, 128), np.float32); lo[np.arange(64), np.arange(64)] = 1
    hi = np.zeros((128, 128), np.float32); hi[np.arange(64, 128), np.arange(64, 128)] = 1
    c["idlo"], c["idhi"] = lo, hi
    c["neg"] = np.full((128, 64), -1e30, np.float32)
    sel = np.zeros((2, 256), np.float32); sel[0, :128] = 1; sel[1, 128:] = 1
    c["sel"] = sel
    pos = np.arange(1024)
    inv = 10000.0 ** (-np.arange(16, dtype=np.float64) / 16)
    ar = (pos // 64).astype(np.float64)[:, None] * inv
    ac = (pos % 64).astype(np.float64)[:, None] * inv
    cr, sr, cc, sc = np.cos(ar), np.sin(ar), np.cos(ac), np.sin(ac)
    cosf = np.concatenate([cr, cr, cc, cc], -1).astype(np.float32)
    sinf = np.concatenate([-sr, sr, -sc, sc], -1).astype(np.float32)
    c["ropec"] = np.ascontiguousarray(cosf.reshape(8, 128, 64).transpose(1, 0, 2).reshape(128, 512))
    c["ropes"] = np.ascontiguousarray(sinf.reshape(8, 128, 64).transpose(1, 0, 2).reshape(128, 512))
    for Ln in (256, 1024):
        t = np.arange(Ln, dtype=np.float64) / Ln
        ang = 2.0 * np.pi * t[:, None] * np.arange(1, 17, dtype=np.float64)
        emb = np.concatenate([t[:, None], np.cos(ang), np.sin(ang)], -1).astype(np.float32)
        t = t.astype(np.float32)
        c[f"embT{Ln}"] = np.ascontiguousarray(emb.T)
        c[f"tcol{Ln}"] = np.ascontiguousarray((-t).reshape(Ln // 128, 128).T)
        tt = np.arange(Ln, dtype=np.float64)[:, None]
        w = np.pi * (2 * np.arange(Ln, dtype=np.float64)[None, :] + 1) / (2 * Ln)
        Cm, Sm = np.cos(tt * w), np.sin(tt * w)
        c[f"Cm{Ln}"] = np.ascontiguousarray(Cm.astype(np.float32)); c[f"Sm{Ln}"] = np.ascontiguousarray(Sm.astype(np.float32))
        c[f"CmT{Ln}"] = np.ascontiguousarray(Cm.T.astype(np.float32)); c[f"SmT{Ln}"] = np.ascontiguousarray(Sm.T.astype(np.float32))
    return c


def _na_table(rel_bias):
    rb = np.asarray(rel_bias, np.float32)
    i = np.arange(2)[:, None, None, None, None]
    kc = np.arange(64)[None, :, None, None, None]
    ep = np.arange(16)[None, None, :, None, None]
    a = np.arange(2)[None, None, None, :, None]
    qc = np.arange(64)[None, None, None, None, :]
    e = ep - a
    dr = e + i - 7
    cs = np.clip(qc - 8, 0, 48)
    shp = (2, 64, 16, 2, 64)
    ok = np.broadcast_to((e >= 0) & (e <= 14) & (np.abs(dr) <= 7) & (kc >= cs) & (kc < cs + 16), shp)
    ri = np.broadcast_to(np.clip(dr + 7, 0, 14), shp)
    ci = np.broadcast_to(np.clip(kc - qc, -15, 15) + 15, shp)
    tab = np.empty((2, 64, 16, 16, 2, 64), np.float32)
    for h in range(16):
        tab[:, :, h] = np.where(ok, rb[h][ri, ci], np.float32(-1e30))
    return np.ascontiguousarray(tab.reshape(128, 16 * 2048))


_NC_CACHE = {}


def kernel(**inp):
    n_layers = int(inp.pop("_n_layers", 4))
    f = lambda a: np.ascontiguousarray(np.asarray(a, dtype=np.float32))
    if n_layers not in _NC_CACHE:
        kb = KB(n_layers)
        _NC_CACHE[n_layers] = kb.build()
    nc = _NC_CACHE[n_layers]
    consts = _host_consts()
    shared = dict(consts)
    shared["ada_w"] = f(inp["ada_w"])
    ab = f(inp["ada_b"])
    shared["ada_b"] = ab
    shared["adab_col"] = np.ascontiguousarray(np.repeat(ab.reshape(4, 48, 128).transpose(0, 2, 1)[:, :, :, None], 2, axis=3).reshape(4, 128, 96))
    shared["ln_g"], shared["ln_b"] = f(inp["ln_g"]), f(inp["ln_b"])
    shared["lng_col"] = np.ascontiguousarray(f(inp["ln_g"]).reshape(4, 2, 8, 128).transpose(3, 0, 1, 2).reshape(128, 64))
    shared["lnb_col"] = np.ascontiguousarray(f(inp["ln_b"]).reshape(4, 2, 8, 128).transpose(3, 0, 1, 2).reshape(128, 64))
    shared["mlp_w1"], shared["mlp_w2"] = f(inp["mlp_w1"]), f(inp["mlp_w2"])
    shared["da_w_qkv"], shared["da_w_o"] = f(inp["da_w_qkv"])[0], f(inp["da_w_o"])[0]
    shared["da_lambda"] = f(inp["da_lambda"]).reshape(1, 256)
    shared["da_subln_g"] = f(inp["da_subln_g"]).reshape(1, 128)
    shared["na_w_qkv"], shared["na_w_o"] = f(inp["na_w_qkv"])[0], f(inp["na_w_o"])[0]
    shared["na_tab"] = _na_table(f(inp["na_rel_bias"])[0])
    shared["gq_w_qkv"], shared["gq_w_o"] = f(inp["gq_w_qkv"])[0], f(inp["gq_w_o"])[0]
    shared["gq_q_norm"], shared["gq_k_norm"] = f(inp["gq_q_norm"]).reshape(1, 64), f(inp["gq_k_norm"]).reshape(1, 64)
    shared["hy_w_in"], shared["hy_w_o"] = f(inp["hy_w_in"])[0], f(inp["hy_w_o"])[0]
    shared["hy_sw_col"] = np.ascontiguousarray(f(inp["hy_short_w"])[0].reshape(3, 24, 128).transpose(2, 0, 1).reshape(128, 72))
    shared["hy_sb_col"] = np.ascontiguousarray(f(inp["hy_short_b"])[0].reshape(24, 128).T)
    shared["hy_fb_col"] = np.ascontiguousarray(f(inp["hy_filter_bias"])[0].reshape(2, 8, 128).transpose(2, 0, 1).reshape(128, 16))
    shared["hy_w1"], shared["hy_w2"], shared["hy_w3"] = f(inp["hy_ffn_w1"])[0], f(inp["hy_ffn_w2"])[0], f(inp["hy_ffn_w3"])[0]
    shared["hy_cols"] = np.ascontiguousarray(np.stack([f(inp["hy_ffn_b1"])[0], f(inp["hy_ffn_b2"])[0], f(inp["hy_ffn_freq"])[0]], -1))
    shared["hy_log_decay"] = f(inp["hy_log_decay"]).reshape(1, 4096)
    xp, xs, c, cctx = f(inp["x_prompt"]), f(inp["x_sample"]), f(inp["c"]), f(inp["c_ctx"])
    in_maps = []
    for b in range(8):
        m = dict(shared)
        m["xin"] = np.ascontiguousarray(np.concatenate([xp[2 * b].reshape(256, D), xp[2 * b + 1].reshape(256, D), xs[b]], 0))
        cv = np.stack([cctx.reshape(8, 128), c[b].reshape(8, 128)], -1)
        m["cvec"] = np.ascontiguousarray(cv.transpose(1, 0, 2).reshape(128, 16))
        m["ck_da"], m["cv_da"] = f(inp["cache_da_k"][b, 0]).reshape(256, D), f(inp["cache_da_v"][b, 0]).reshape(256, D)
        m["ck_na"], m["cv_na"] = f(inp["cache_na_k"][b, 0]).reshape(256, D), f(inp["cache_na_v"][b, 0]).reshape(256, D)
        m["ck_gq"], m["cv_gq"] = f(inp["cache_gq_k"][b, 0]).reshape(256, 256), f(inp["cache_gq_v"][b, 0]).reshape(256, 256)
        in_maps.append(m)
    res = run_bass_kernel_spmd(nc, in_maps, core_ids=list(range(8)))
    R = res.results
    y = np.stack([r["y"] for r in R])
    y_p = y[:, :512].reshape(16, 256, D)
    y_s = y[:, 512:].reshape(8, 1024, D)

    def st(name, H, dh):
        a = np.stack([r[name] for r in R])
        return np.ascontiguousarray(a.reshape(16, 1, 256, H, dh))
    return (np.ascontiguousarray(y_p), np.ascontiguousarray(y_s),
            st("st_da_k", 8, 128), st("st_da_v", 8, 128), st("st_na_k", 16, 64), st("st_na_v", 16, 64),
            st("st_gq_k", 4, 64), st("st_gq_v", 4, 64))


TWO_PI = float(2 * np.pi)


def _hy_filters(self):
    d = self.dram
    S = self.S
    ph = ExitStack()
    sb = lambda n, s_, dt: self.sb(n, s_, dt, ph)
    embT = sb("embT", [33, 1024], F32)
    w1s = sb("hw1", [33, 64], F32)
    w2s = sb("hw2", [64, 64], F32)
    w3s = sb("hw3", [64, 2, 512], BF16)
    cols = sb("hcols", [64, 4], F32)
    fb = sb("hfb", [64, 2], F32)
    negpi = sb("negpi", [64, 1], F32)
    h1T = sb("h1T", [64, 1024], F32)
    h2T = sb("h2T", [64, 1024], F32)
    ktmp = sb("ktmp", [64, 512], F32)
    ktmp2 = sb("ktmp2", [64, 512], F32)
    hpi = sb("hpi", [64, 1], F32)
    h2b = sb("h2b", [64, 1024], BF16)
    ldec = sb("ldec", [128, 2, 512], F32)
    win = sb("win", [128, 2, 512], F32)
    FB = sb("FB", [128, 2, 512], F32)
    ee = sb("ee", [128, 2, 8, 512], BF16)
    dd = sb("dd", [128, 2, 8, 512], BF16)
    ringf = sb("ringf", [128, 1, 8, 512], BF16)
    self.slots = self.slots[:NSLOT] + [(ringf[:, 0], ("ringf", 0))]
    hst = sb("hst", [128, 2, 512], BF16)
    tcol = sb("tcol", [128, 8], F32)
    S.dma("sp", w1s[:], d["hy_w1"], wr=["hw1"])
    S.dma("sp", w2s[:], d["hy_w2"], wr=["hw2"])
    S.dma("sp", cols[:, 0:3], d["hy_cols"], wr=["hcols"])
    self.op("dve", lambda e: e.memset(negpi[:], -float(np.pi)), wr=["negpi"])
    self.op("dve", lambda e: e.memset(hpi[:], float(np.pi / 2)), wr=["negpi"])
    for j in range(2):
        self.op("dve", lambda e, j=j: e.tensor_tensor(fb[:, j:j + 1], cols[:, j:j + 1], cols[:, 2:3], ALU.mult),
                rd=["hcols"], wr=["hfb"])
    self.hsi = 0
    for Ln in (256, 1024):
        nt = Ln // 128
        S.dma("sp", embT[:, 0:Ln], d[f"embT{Ln}"], wr=["embT"])
        S.dma("sp", tcol[:, 0:nt], d[f"tcol{Ln}"], wr=["tcol"])
        for (wsrc, wk, src, sk, dst, dk, j) in ((w1s, "hw1", embT, "embT", h1T, "h1T", 0), (w2s, "hw2", h1T, "h1T", h2T, "h2T", 1)):
            for n0 in range(0, Ln, 512):
                n = min(512, Ln - n0)
                b = self.newbank()
                kdim = 33 if j == 0 else 64
                self.mmg(self.PS[b][0:64, 0:n], [(wsrc[0:kdim, :], src[0:kdim, n0:n0 + n])], rd=[wk, sk], wr=[("ps", b)])
                dv_ = dst[:, n0:n0 + n]
                self.op("act", lambda e, b=b, n=n, dv_=dv_, j=j: e.activation(dv_, self.PS[b][0:64, 0:n], AF.Identity,
                                                                          bias=fb[:, j:j + 1], scale=cols[:, 2:3]),
                        rd=[("ps", b), "hfb", "hcols"], wr=[dk])
                kt = ktmp[:, 0:n]
                k2 = ktmp2[:, 0:n]
                self.op("act", lambda e, dv_=dv_, kt=kt: e.activation(kt, dv_, AF.Sin, scale=0.125), rd=[dk], wr=["ktmp"])
                self.op("dve", lambda e, kt=kt: e.tensor_tensor(kt, kt, kt, ALU.mult), rd=["ktmp"], wr=["ktmp"])
                self.op("dve", lambda e, kt=kt: e.tensor_scalar(kt, kt, -2.0, 1.0, ALU.mult, ALU.add), rd=["ktmp"], wr=["ktmp"])
                self.op("act", lambda e, dv_=dv_: e.activation(dv_, dv_, AF.Sin, scale=0.25), rd=[dk], wr=[dk])
                self.op("dve", lambda e, dv_=dv_, k2=k2: e.tensor_tensor(k2, dv_, dv_, ALU.mult), rd=[dk], wr=["ktmp2"])
                self.op("dve", lambda e, k2=k2: e.tensor_scalar(k2, k2, -2.0, 1.0, ALU.mult, ALU.add), rd=["ktmp2"], wr=["ktmp2"])
                self.op("dve", lambda e, dv_=dv_, kt=kt: e.tensor_tensor(kt, dv_, kt, ALU.mult), rd=[dk, "ktmp"], wr=["ktmp"])
                self.op("dve", lambda e, dv_=dv_, kt=kt, k2=k2: e.scalar_tensor_tensor(dv_, kt, 4.0, k2, ALU.mult, ALU.mult),
                        rd=["ktmp", "ktmp2"], wr=[dk])
            if j == 1:
                self.op("act", lambda e: e.activation(h2b[:, 0:Ln], h2T[:, 0:Ln], AF.Copy), rd=["h2T"], wr=["h2b"])
            self.ada0_tick()
        def gen_e(dblk, o):
            cF = o * 2048 + dblk * 512
            cB = cF + 1024
            for j, c0 in enumerate((cF, cB)):
                S.dma("pool", w3s[:, j, :], d["hy_w3"][:, c0:c0 + 512], wr=[("hw3", j)])
                S.dma("sp", ldec[:, j, :], d["hy_log_decay"][:, c0:c0 + 512].partition_broadcast(128), wr=[("ldec", j)])
                self.op("act", lambda e, j=j: e.activation(ldec[:, j, :], ldec[:, j, :], AF.Exp), rd=[("ldec", j)], wr=[("ldec", j)])
            for tch in range(nt):
                for j in range(2):
                    b = self.newbank()
                    self.mmg(self.PS[b][:], [(h2b[:, tch * 128:(tch + 1) * 128], w3s[:, j, :])], rd=["h2b", ("hw3", j)], wr=[("ps", b)])
                    self.op("act", lambda e, j=j, tch=tch: e.activation(win[:, j, :], ldec[:, j, :], AF.Exp, scale=tcol[:, tch:tch + 1]),
                            rd=[("ldec", j), "tcol"], wr=[("win", j)])
                    self.op("dve", lambda e, j=j, b=b: e.tensor_tensor(FB[:, j, :], self.PS[b][:], win[:, j, :], ALU.mult),
                            rd=[("ps", b), ("win", j)], wr=[("FB", j)])
                if tch == 0:
                    self.op("dve", lambda e: e.memset(FB[0:1, 1, :], 0.0), rd=[("FB", 1)], wr=[("FB", 1)])
                self.op("dve", lambda e, tch=tch, o=o: e.tensor_tensor(ee[:, o, tch, :], FB[:, 0, :], FB[:, 1, :], ALU.add),
                        rd=[("FB", 0), ("FB", 1)], wr=[("ee", o, tch)])
                self.op("dve", lambda e, tch=tch, o=o: e.tensor_tensor(dd[:, o, tch, :], FB[:, 1, :], FB[:, 0, :], ALU.subtract),
                        rd=[("FB", 0), ("FB", 1)], wr=[("dd", o, tch)])
                yield
            self.ada0_tick()

        def gen_H(dblk, o):
            fbw = min(512, Ln)
            for fb0 in range(0, Ln, fbw):
                sl = {}
                for mi, mname in enumerate(("Cm", "Sm")):
                    sl[mi] = self.wload(d[f"{mname}{Ln}"][:, fb0:fb0 + fbw], fbw, nk=nt)
                for fl in range(fbw // 128):
                    fch = fb0 // 128 + fl
                    for mi, src, skn in ((0, ee, "ee"), (1, dd, "dd")):
                        slot, rk = sl[mi]
                        b = self.newbank()
                        self.mmg(self.PS[b][:], [(slot[:, tch, fl * 128:(fl + 1) * 128], src[:, o, tch, :]) for tch in range(nt)],
                                 rd=[rk] + [(skn, o, tch) for tch in range(nt)], wr=[("ps", b)])
                        hp = self.hsi % 2
                        self.hsi += 1
                        self.op("act", lambda e, b=b, hp=hp: e.activation(hst[:, hp, :], self.PS[b][:], AF.Copy, scale=1.0 / Ln),
                                rd=[("ps", b)], wr=[("hst", hp)])
                        S.dma("sp", d[f"H{Ln}"][o, mi, fch * 128:(fch + 1) * 128, dblk * 512:(dblk + 1) * 512], hst[:, hp, :],
                              rd=[("hst", hp)], wr=[("Hd", Ln, o, mi, fch, dblk)])
                    yield
            self.ada0_tick()

        def run(*gens):
            gens = list(gens)
            while gens:
                for g_ in list(gens):
                    try:
                        next(g_)
                    except StopIteration:
                        gens.remove(g_)

        order = [(dblk, o) for dblk in range(2) for o in range(2)]
        run(gen_e(*order[0]))
        for qi_ in range(len(order)):
            if qi_ + 1 < len(order):
                run(gen_H(*order[qi_]), gen_e(*order[qi_ + 1]))
            else:
                run(gen_H(*order[qi_]))
    names = ("embT", "hw1", "hw2", "hw3", "hcols", "hfb", "negpi", "h1T", "h2T", "ldec", "win", "FB", "ee", "dd", "hst", "tcol", "ktmp", "ringf", "ktmp2", "hpi", "h2b")
    S.retire(lambda k: (k in names) or (isinstance(k, tuple) and k[0] in names))
    self.slots = self.slots[:NSLOT]
    self.ring_i = self.ring_i % NSLOT
    ph.close()


def _hy_layer(self, i):
    d = self.dram
    S = self.S
    ph = ExitStack()
    sb = lambda n, s_, dt: self.sb(n, s_, dt, ph)
    zT = sb("zT", [128, 4, 1024], BF16)
    gT = sb("gT", [128, 4, 1024], BF16)
    ztok = sb("ztok", [128, 8, 512], BF16)
    Yre = sb("Yre", [128, 8, 512], BF16)
    Yng = sb("Yng", [128, 8, 512], BF16)
    z2T = sb("z2T", [128, 8, 1024], BF16)
    v32 = sb("v32", [128, 2, 1024], F32)
    Ht = sb("Ht", [128, 2, 2, 512], BF16)
    t32 = sb("t32", [128, 4, 512], F32)
    swc = sb("swc", [128, 72], F32)
    sbc = sb("sbc", [128, 24], F32)
    fbc = sb("fbc", [128, 16], F32)
    S.dma("sp", swc[:], d["hy_sw_col"], wr=["swc"])
    S.dma("sp", sbc[:], d["hy_sb_col"], wr=["sbc"])
    S.dma("sp", fbc[:], d["hy_fb_col"], wr=["fbc"])
    self.lazy_flush()
    self.load_lnbc(i, 0)
    self.gate_bcast(0)
    w_in = d["hy_w_in"]
    hti = 0
    tti = 0
    self.hv_i = 0
    for (t0, ntile, Ln) in ((0, 2, 256), (2, 2, 256), (4, 8, 1024)):
        if t0 == 2:
            self.ln_flush()
        tok0 = t0 * 128
        nt = Ln // 128
        nb = max(1, Ln // 512)
        bw = min(512, Ln)
        hk = [("hT", t) for t in range(t0, t0 + ntile)]

        def inproj(part, dblk, dstT, dname):
            slot, rk = self.wload(w_in[:, part * 1024 + dblk * 512: part * 1024 + (dblk + 1) * 512], 512)
            pend = []

            def fin(item):
                ch, pp, pkeys, vv, kv, fcg, hb = item
                w2_ = swc[:, 48 + fcg:48 + fcg + 1]
                self.op("dve", lambda e: e.scalar_tensor_tensor(dstT[:, ch, 0:Ln - 1], pp[:, 1:Ln], w2_, vv[:, 0:Ln - 1], ALU.mult, ALU.add),
                        rd=pkeys + [kv, "swc"], wr=[(dname, ch)])
                self.op("act", lambda e: e.activation(dstT[:, ch, Ln - 1:Ln], vv[:, Ln - 1:Ln], AF.Copy), rd=[kv], wr=[(dname, ch)])
                for b_ in hb:
                    self.held.discard(b_)

            for ch in range(4):
                fcg = part * 8 + dblk * 4 + ch
                if nb == 2:
                    b0, pp = self.newpair()
                    hb = [b0, b0 + 1]
                else:
                    b0 = self.newbank()
                    pp = self.PS[b0]
                    hb = [b0]
                for b_ in hb:
                    self.held.add(b_)
                pkeys = [("ps", b_) for b_ in hb]
                for tb in range(nb):
                    self.mmg(pp[:, tb * bw:(tb + 1) * bw], [(slot[:, kc, ch * 128:(ch + 1) * 128], self.hT[:, kc, tok0 + tb * bw: tok0 + (tb + 1) * bw])
                                                             for kc in range(8)], rd=hk + [rk], wr=pkeys)
                vp = self.hv_i % 2
                self.hv_i += 1
                vv, kv = v32[:, vp, :], ("v32", vp)
                w0 = swc[:, fcg:fcg + 1]
                w1_ = swc[:, 24 + fcg:24 + fcg + 1]
                self.op("act", lambda e, pp=pp, vv=vv, w1_=w1_, fcg=fcg: e.activation(vv[:, 0:Ln], pp[:, 0:Ln], AF.Identity, bias=sbc[:, fcg:fcg + 1], scale=w1_),
                        rd=pkeys + ["swc", "sbc"], wr=[kv])
                self.op("dve", lambda e, pp=pp, vv=vv, w0=w0: e.scalar_tensor_tensor(vv[:, 1:Ln], pp[:, 0:Ln - 1], w0, vv[:, 1:Ln], ALU.mult, ALU.add),
                        rd=pkeys + [kv, "swc"], wr=[kv])
                pend.append((ch, pp, pkeys, vv, kv, fcg, hb))
                if len(pend) > 1:
                    fin(pend.pop(0))
            while pend:
                fin(pend.pop(0))

        for dblk in range(2):
            self.mark(f"hy L{Ln} t0={t0} dblk{dblk} inproj z")
            inproj(0, dblk, zT, "zT")
            for o in range(2):
                self.mark(f"hy L{Ln} t0={t0} dblk{dblk} o{o} inproj g")
                inproj(1 + o, dblk, gT, "gT")
                self.mark(f"hy L{Ln} t0={t0} dblk{dblk} o{o} ztok+fwd")
                for tt in range(nt):
                    def dst(pv, pk, tt=tt):
                        self.op("act", lambda e: e.activation(ztok[:, tt, :].rearrange("p (c f) -> p c f", c=4), pv, AF.Copy),
                                rd=[pk], wr=[("ztok", tt)])
                    b = self.newbank()
                    pb = self.PSB[b]
                    for ch in range(4):
                        self.op("pe", lambda e, ch=ch, tt=tt: e.transpose(pb[:, ch * 128:(ch + 1) * 128], zT[:, ch, tt * 128:(tt + 1) * 128], self.ident[:]),
                                rd=[("zT", ch), "ident"], wr=[("ps", b)], inc=(ch == 3))
                    dst(pb[:, 0:512].rearrange("p (c t) -> p c t", c=4), ("ps", b))
                for fb0 in range(0, Ln, bw):
                    sl = [self.wload(d[f"{mn}{Ln}"][:, fb0:fb0 + bw], bw, nk=nt) for mn in ("Cm", "Sm")]
                    for fl in range(bw // 128):
                        fch = fb0 // 128 + fl
                        hp = hti % 2
                        hti += 1
                        for mi in range(2):
                            S.dma("sp", Ht[:, hp, mi, :], d[f"H{Ln}"][o, mi, fch * 128:(fch + 1) * 128, dblk * 512:(dblk + 1) * 512],
                                  rd=[("Hd", Ln, o, mi, fch, dblk)], wr=[("Ht", hp, mi)])
                        bz = []
                        for mi in range(2):
                            slot, rk = sl[mi]
                            b = self.newbank()
                            self.held.add(b)
                            self.mmg(self.PS[b][:], [(slot[:, tch, fl * 128:(fl + 1) * 128], ztok[:, tch, :]) for tch in range(nt)],
                                     rd=[rk] + [("ztok", tch) for tch in range(nt)], wr=[("ps", b)])
                            bz.append(b)
                        Zc, Zs = self.PS[bz[0]], self.PS[bz[1]]
                        kz = [("ps", bz[0]), ("ps", bz[1])]
                        Hc, Hs_ = Ht[:, hp, 0, :], Ht[:, hp, 1, :]
                        prods = ((0, Zc, kz[0], Hc, 0), (1, Zs, kz[1], Hs_, 1), (2, Zs, kz[1], Hc, 0), (3, Zc, kz[0], Hs_, 1))
                        for (pi_, Z_, kz_, H_, hm) in prods:
                            self.op("dve", lambda e, pi_=pi_, Z_=Z_, H_=H_: e.tensor_tensor(t32[:, pi_, :], Z_, H_, ALU.mult),
                                    rd=[kz_, ("Ht", hp, hm)], wr=[("t32", pi_)])
                        self.op("dve", lambda e, fch=fch: e.tensor_tensor(Yre[:, fch, :], t32[:, 0, :], t32[:, 1, :], ALU.add),
                                rd=[("t32", 0), ("t32", 1)], wr=[("Yre", fch)])
                        self.op("dve", lambda e, fch=fch: e.tensor_tensor(Yng[:, fch, :], t32[:, 2, :], t32[:, 3, :], ALU.subtract),
                                rd=[("t32", 2), ("t32", 3)], wr=[("Yng", fch)])
                        for b in bz:
                            self.held.discard(b)
                self.mark(f"hy L{Ln} t0={t0} dblk{dblk} o{o} inverse")
                for tb in range(nb):
                    sl = [self.wload(d[f"{mn}{Ln}"][:, tb * bw:(tb + 1) * bw], bw, nk=nt) for mn in ("CmT", "SmT")]
                    for ch in range(4):
                        b = self.newbank()
                        pairs = []
                        for mi, Y in ((0, Yre), (1, Yng)):
                            slot, rk = sl[mi]
                            pairs += [(Y[:, fch, ch * 128:(ch + 1) * 128], slot[:, fch, 0:bw]) for fch in range(nt)]
                        self.mmg(self.PS[b][:, 0:bw], pairs, rd=[sl[0][1], sl[1][1]] + [("Yre", f_) for f_ in range(nt)] + [("Yng", f_) for f_ in range(nt)],
                                 wr=[("ps", b)])
                        tp = tti % 4
                        tti += 1
                        ta, ka = t32[:, tp, 0:bw], ("t32", tp)
                        fcol = fbc[:, o * 8 + dblk * 4 + ch: o * 8 + dblk * 4 + ch + 1]
                        zsl = zT[:, ch, tb * bw:(tb + 1) * bw]
                        self.op("dve", lambda e, b=b, ta=ta, zsl=zsl, fcol=fcol: e.scalar_tensor_tensor(ta, zsl, fcol, self.PS[b][:, 0:bw], ALU.mult, ALU.add),
                                rd=[("zT", ch), "fbc", ("ps", b)], wr=[ka])
                        if o == 0:
                            odst, okey = zsl, ("zT", ch)
                        else:
                            odst, okey = z2T[:, dblk * 4 + ch, tb * bw:(tb + 1) * bw], ("z2T", dblk * 4 + ch)
                        self.op("dve", lambda e, ta=ta, odst=odst, ch=ch, tb=tb: e.tensor_tensor(odst, ta, gT[:, ch, tb * bw:(tb + 1) * bw], ALU.mult),
                                rd=[ka, ("gT", ch)], wr=[okey])
        self.mark(f"hy L{Ln} t0={t0} wo+ln")
        wo = d["hy_w_o"]
        slots = [self.wload(wo[:, hf * 512:(hf + 1) * 512], 512) for hf in range(2)]
        items = []
        for lt in range(ntile):
            t = t0 + lt
            bs = []
            for hf in range(2):
                b = self.newbank()
                self.held.add(b)
                slot, rk = slots[hf]
                self.mmg(self.PS[b][:], [(z2T[:, kc, lt * 128:(lt + 1) * 128], slot[:, kc, :]) for kc in range(8)],
                         rd=[("z2T", kc) for kc in range(8)] + [rk], wr=[("ps", b)])
                bs.append(b)
            items.append((t, [self.PS[b][:] for b in bs], [("ps", b) for b in bs], 1))
            if len(items) == 2 or lt == ntile - 1:
                self.ln_tiles(items)
                for it in items:
                    for kb_ in it[2]:
                        self.held.discard(kb_[1])
                items = []
    names = ("zT", "gT", "ztok", "Yre", "Yng", "z2T", "v32", "Ht", "t32", "swc", "sbc", "fbc")
    S.retire(lambda k: (k in names) or (isinstance(k, tuple) and k[0] in names))
    ph.close()


KB.hyena_filters = _hy_filters
KB.hyena_layer = _hy_layer
```
